# Optimizing a Trainium2 kernel written in Bass

```python
import jax, jax.numpy as jnp
from jax import lax
import numpy as np

D_MODEL = 1024
BATCH = 4
SEQ = 4096
DEPTH = 1

GRID_W = 64
CTX_LEN = 256
MLA_HEADS = 8
MLA_NOPE = 64
MLA_ROPE = 32
MLA_V = 64
Q_LORA = 256
KV_LORA = 128
MLA_WIDTH = MLA_HEADS * MLA_V
NA_HEADS = 8
NA_DIM = 64
NA_KR = 8
NA_KC = 16
NA_WIDTH = NA_HEADS * NA_DIM

ROPE_THETA = 10000.0
Q_BLOCK = 128
EPS = 1e-6

KV_SPLITS = (KV_LORA, MLA_ROPE, NA_WIDTH, NA_WIDTH)
Q_SPLITS = (Q_LORA, MLA_WIDTH, NA_WIDTH, NA_WIDTH, D_MODEL, D_MODEL)
KV_COLS = KV_LORA + MLA_ROPE + 2 * NA_WIDTH
IN_WIDTH = KV_COLS + Q_LORA + MLA_WIDTH + 2 * NA_WIDTH + 2 * D_MODEL

kernel_name = "hybrid_mla_natten_gated_dit_layer"


def rmsnorm(x, g):
    xf = x.astype(jnp.float32)
    y = xf * lax.rsqrt(jnp.mean(xf * xf, axis=-1, keepdims=True) + EPS)
    return (y * g.astype(jnp.float32)).astype(x.dtype)


def split_cols(p, sizes):
    idx = np.cumsum(np.array(sizes))[:-1].tolist()
    return jnp.split(p, idx, axis=-1)


def heads(t, n):
    b, s, _ = t.shape
    return t.reshape(b, s, n, -1)


def rope_1d(x, pos):
    half = x.shape[-1] // 2
    freqs = ROPE_THETA ** (-jnp.arange(half, dtype=jnp.float32) / half)
    ang = pos.astype(jnp.float32)[:, None] * freqs[None, :]
    cos = jnp.cos(ang)[None, :, None, :]
    sin = jnp.sin(ang)[None, :, None, :]
    x1 = x[..., :half].astype(jnp.float32)
    x2 = x[..., half:].astype(jnp.float32)
    out = jnp.concatenate([x1 * cos - x2 * sin, x2 * cos + x1 * sin], axis=-1)
    return out.astype(x.dtype)


def rope_2d(x, rows, cols):
    d = x.shape[-1] // 2
    return jnp.concatenate([rope_1d(x[..., :d], rows), rope_1d(x[..., d:], cols)], axis=-1)


def mla_q(cq, g_cq, w_uq):
    b, s, _ = cq.shape
    q = (rmsnorm(cq, g_cq) @ w_uq).reshape(b, s, MLA_HEADS, MLA_NOPE + MLA_ROPE)
    return q[..., :MLA_NOPE], q[..., MLA_NOPE:]


def mla_kv(ckv, g_ckv, w_ukv):
    b, s, _ = ckv.shape
    kv = (rmsnorm(ckv, g_ckv) @ w_ukv).reshape(b, s, MLA_HEADS, MLA_NOPE + MLA_V)
    return kv[..., :MLA_NOPE], kv[..., MLA_NOPE:]


def block_attention(q, k, v, scale):
    b, s, h, dk = q.shape
    dv = v.shape[-1]
    nb = s // Q_BLOCK
    qb = q.reshape(b, nb, Q_BLOCK, h, dk).transpose(1, 0, 2, 3, 4)

    def one(qblk):
        sc = jnp.einsum('bqhd,bkhd->bhqk', qblk, k, preferred_element_type=jnp.float32) * scale
        p = jax.nn.softmax(sc, axis=-1).astype(v.dtype)
        return jnp.einsum('bhqk,bkhd->bqhd', p, v)

    o = lax.map(one, qb)
    return o.transpose(1, 0, 2, 3, 4).reshape(b, s, h * dv)


def neighborhood_attention(q, k, v, kc, vc, rpb):
    b, s, h, d = q.shape
    rows = s // GRID_W
    kr = min(NA_KR, rows)
    qg = q.reshape(b, rows, GRID_W, h, d)
    kg = k.reshape(b, rows, GRID_W, h, d)
    vg = v.reshape(b, rows, GRID_W, h, d)
    col = jnp.arange(GRID_W)
    c_start = jnp.clip(col - NA_KC // 2, 0, GRID_W - NA_KC)
    c_idx = c_start[:, None] + jnp.arange(NA_KC)[None, :]
    c_off = c_idx - col[:, None] + (NA_KC - 1)
    scale = d ** -0.5

    def one_row(r):
        r_start = jnp.clip(r - kr // 2, 0, rows - kr)
        k_slab = lax.dynamic_slice_in_dim(kg, r_start, kr, axis=1)
        v_slab = lax.dynamic_slice_in_dim(vg, r_start, kr, axis=1)
        k_win = k_slab[:, :, c_idx]
        v_win = v_slab[:, :, c_idx]
        qr = lax.dynamic_index_in_dim(qg, r, axis=1, keepdims=False)
        r_off = r_start + jnp.arange(kr) - r + (NA_KR - 1)
        bias = rpb[:, r_off[:, None, None], c_off[None, :, :]]
        bias = bias.transpose(0, 2, 1, 3).reshape(h, GRID_W, kr * NA_KC).astype(jnp.float32)
        s_win = jnp.einsum('bwhd,brwchd->bhwrc', qr, k_win, preferred_element_type=jnp.float32)
        s_win = s_win.reshape(b, h, GRID_W, kr * NA_KC) * scale + bias[None]
        s_ctx = jnp.einsum('bwhd,blhd->bhwl', qr, kc, preferred_element_type=jnp.float32) * scale
        p = jax.nn.softmax(jnp.concatenate([s_win, s_ctx], axis=-1), axis=-1).astype(v.dtype)
        p_win = p[..., :kr * NA_KC].reshape(b, h, GRID_W, kr, NA_KC)
        p_ctx = p[..., kr * NA_KC:]
        return (jnp.einsum('bhwrc,brwchd->bwhd', p_win, v_win)
                + jnp.einsum('bhwl,blhd->bwhd', p_ctx, vc))

    o = lax.map(one_row, jnp.arange(rows))
    return o.transpose(1, 0, 2, 3, 4).reshape(b, s, h * d)


def setup_inputs(seed: int = 0) -> dict:
    key = jax.random.key(seed)
    ks = jax.random.split(key, 20)
    n = jax.random.normal
    D = D_MODEL
    return {
        "x": n(ks[0], (BATCH, SEQ, D), jnp.float32),
        "c": n(ks[1], (BATCH, D), jnp.float32),
        "ctx": n(ks[2], (BATCH, CTX_LEN, D), jnp.float32),
        "c_ctx": n(ks[3], (D,), jnp.float32),
        "w_mod": n(ks[4], (DEPTH, D, 3 * D), jnp.float32) * (0.5 * D ** -0.5),
        "b_mod": n(ks[5], (DEPTH, 3 * D), jnp.float32) * 0.01,
        "norm_g": 1.0 + 0.05 * n(ks[6], (DEPTH, D), jnp.float32),
        "w_in": n(ks[7], (DEPTH, D, IN_WIDTH), jnp.float32) * D ** -0.5,
        "g_cq": 1.0 + 0.05 * n(ks[8], (DEPTH, Q_LORA), jnp.float32),
        "w_uq": n(ks[9], (DEPTH, Q_LORA, MLA_HEADS * (MLA_NOPE + MLA_ROPE)), jnp.float32) * Q_LORA ** -0.5,
        "g_ckv": 1.0 + 0.05 * n(ks[10], (DEPTH, KV_LORA), jnp.float32),
        "w_ukv": n(ks[11], (DEPTH, KV_LORA, MLA_HEADS * (MLA_NOPE + MLA_V)), jnp.float32) * KV_LORA ** -0.5,
        "rpb": 0.02 * n(ks[12], (DEPTH, NA_HEADS, 2 * NA_KR - 1, 2 * NA_KC - 1), jnp.float32),
        "w_oa": n(ks[13], (DEPTH, MLA_WIDTH, D), jnp.float32) * MLA_WIDTH ** -0.5,
        "w_ob": n(ks[14], (DEPTH, NA_WIDTH, D), jnp.float32) * NA_WIDTH ** -0.5,
        "w_out": n(ks[15], (DEPTH, D, D), jnp.float32) * D ** -0.5,
        "final_g": 1.0 + 0.05 * n(ks[16], (D,), jnp.float32),
    }


def reference(x, c, ctx, c_ctx, w_mod, b_mod, norm_g, w_in, g_cq, w_uq, g_ckv, w_ukv,
              rpb, w_oa, w_ob, w_out, final_g):
    b, s, _ = x.shape
    t = jnp.arange(s)
    row_pos = t // GRID_W
    col_pos = t % GRID_W
    mla_scale = (MLA_NOPE + MLA_ROPE) ** -0.5
    na_scale = NA_DIM ** -0.5
    sc = jax.nn.silu(c)
    scc = jax.nn.silu(c_ctx)
    for i in range(DEPTH):
        last = i == DEPTH - 1
        shift, scale, gate = jnp.split((sc @ w_mod[i] + b_mod[i])[:, None, :], 3, axis=-1)
        shift_c, scale_c, gate_c = jnp.split(scc @ w_mod[i] + b_mod[i], 3, axis=-1)
        h = rmsnorm(x, norm_g[i]) * (1 + scale) + shift
        hc = rmsnorm(ctx, norm_g[i]) * (1 + scale_c) + shift_c

        ckv, kr, kb, vb, cq, za, qb, zb, ga, gb = split_cols(h @ w_in[i], KV_SPLITS + Q_SPLITS)
        if last:
            c_ckv, c_kr, c_kb, c_vb = split_cols(hc @ w_in[i][:, :KV_COLS], KV_SPLITS)
        else:
            (c_ckv, c_kr, c_kb, c_vb, c_cq, c_za, c_qb, c_zb, c_ga, c_gb) = split_cols(
                hc @ w_in[i], KV_SPLITS + Q_SPLITS)

        qa_n, qa_r = mla_q(cq, g_cq[i], w_uq[i])
        qa = jnp.concatenate([qa_n, rope_2d(qa_r, row_pos, col_pos)], axis=-1)
        ka_n, va = mla_kv(ckv, g_ckv[i], w_ukv[i])
        ka_r = rope_2d(kr[:, :, None, :], row_pos, col_pos)
        ka = jnp.concatenate([ka_n, jnp.broadcast_to(ka_r, ka_n.shape[:-1] + (MLA_ROPE,))], axis=-1)
        kca_n, vca = mla_kv(c_ckv, g_ckv[i], w_ukv[i])
        kca = jnp.concatenate(
            [kca_n, jnp.broadcast_to(c_kr[:, :, None, :], kca_n.shape[:-1] + (MLA_ROPE,))], axis=-1)
        oa = block_attention(qa, jnp.concatenate([ka, kca], axis=1),
                             jnp.concatenate([va, vca], axis=1), mla_scale)

        ob = neighborhood_attention(heads(qb, NA_HEADS), heads(kb, NA_HEADS), heads(vb, NA_HEADS),
                                    heads(c_kb, NA_HEADS), heads(c_vb, NA_HEADS), rpb[i])

        ya = (oa * jax.nn.silu(za)) @ w_oa[i]
        yb = (ob * jax.nn.silu(zb)) @ w_ob[i]
        y = (jax.nn.sigmoid(ga) * ya + jax.nn.sigmoid(gb) * yb) @ w_out[i]

        if not last:
            qca_n, qca_r = mla_q(c_cq, g_cq[i], w_uq[i])
            oca = block_attention(jnp.concatenate([qca_n, qca_r], axis=-1), kca, vca, mla_scale)
            ocb = block_attention(heads(c_qb, NA_HEADS), heads(c_kb, NA_HEADS),
                                  heads(c_vb, NA_HEADS), na_scale)
            yca = (oca * jax.nn.silu(c_za)) @ w_oa[i]
            ycb = (ocb * jax.nn.silu(c_zb)) @ w_ob[i]
            yc = (jax.nn.sigmoid(c_ga) * yca + jax.nn.sigmoid(c_gb) * ycb) @ w_out[i]
            ctx = ctx + gate_c * yc

        x = x + gate * y
    return rmsnorm(x, final_g)
```

```python
import numpy as np
from contextlib import ExitStack
import concourse.bass as bass
import concourse.mybir as mybir
from concourse.bass_utils import run_bass_kernel_spmd

F32 = mybir.dt.float32
BF16 = mybir.dt.bfloat16
AF = mybir.ActivationFunctionType
ALU = mybir.AluOpType

D = 1024
SEQ = 4096
CTXL = 256
GRID_W = 64
NTILE = 34
NKEEP = 20
NKEY = NTILE * 128
NOWN = 2048
EPS = 1e-6
MLA_SCALE = 96.0 ** -0.5
NA_SCALE = 64.0 ** -0.5
NEG = -30000.0
NVAR = 13

DEBUG_PHASE = None

COMPUTE = ("pe", "act", "dve", "pool")
QUEUES = ("sp", "act", "pool")
ALLENG = ("pe", "act", "dve", "pool", "sp")
NDMASEM = 12


class Op:
    __slots__ = ("eng", "fn", "deps", "sig", "count", "is_dma", "sem", "val", "prev_dma")

    def __init__(self, eng, fn, is_dma=False):
        self.eng = eng
        self.fn = fn
        self.deps = []
        self.sig = False
        self.count = None
        self.is_dma = is_dma
        self.sem = None
        self.val = None
        self.prev_dma = None


class Prog:
    def __init__(self):
        self.ops = {e: [] for e in ALLENG}
        self.order = []
        self.last_w = {}
        self.readers = {}
        self.ndma = {q: 0 for q in QUEUES}
        self.dma_hist = {q: [] for q in QUEUES}
        self.pending_barrier = {}
        self.out_dmas = []

    def _track(self, op, reads, writes):
        deps = []
        for k in reads:
            w = self.last_w.get(k)
            if w is not None:
                deps.append((w, "raw"))
        for k in writes:
            w = self.last_w.get(k)
            if w is not None:
                deps.append((w, "waw"))
            for r in self.readers.get(k, ()):
                deps.append((r, "war"))
        for k in reads:
            self.readers.setdefault(k, []).append(op)
        for k in writes:
            self.last_w[k] = op
            self.readers[k] = []
        seen = set()
        for d, kind in deps:
            if d is op or id(d) in seen:
                continue
            if (not d.is_dma) and (not op.is_dma) and d.eng == op.eng:
                if kind != "raw" or op.eng == "pe":
                    continue
            seen.add(id(d))
            op.deps.append(d)
        b = self.pending_barrier.pop(op.eng, None)
        if b:
            for d in b:
                if d is op or id(d) in seen:
                    continue
                if d.eng == op.eng and not d.is_dma and not op.is_dma:
                    continue
                seen.add(id(d))
                op.deps.append(d)

    def op(self, eng, fn, reads=(), writes=()):
        o = Op(eng, fn)
        self._track(o, reads, writes)
        self.ops[eng].append(o)
        self.order.append(o)
        return o

    def dma(self, queue, out, in_, reads=(), writes=(), is_output=False):
        o = Op(queue, lambda e: e.dma_start(out=out, in_=in_), is_dma=True)
        k = self.ndma[queue]
        self.ndma[queue] = k + 1
        o.sem = (queue, k % NDMASEM)
        o.val = 16 * (k // NDMASEM + 1)
        hist = self.dma_hist[queue]
        if k >= NDMASEM:
            o.prev_dma = hist[k - NDMASEM]
        hist.append(o)
        self._track(o, reads, writes)
        if o.prev_dma is not None and all(d is not o.prev_dma for d in o.deps):
            o.deps.append(o.prev_dma)
        self.ops[queue].append(o)
        self.order.append(o)
        if is_output:
            self.out_dmas.append(o)
        return o

    def barrier(self):
        last = []
        for e in ALLENG:
            for o in reversed(self.ops[e]):
                if not o.is_dma:
                    last.append(o)
                    break
        for q in QUEUES:
            last.extend(self.dma_hist[q][-NDMASEM:])
        for e in ALLENG:
            self.pending_barrier[e] = list(last)

    def emit(self, sems, dsems):
        for o in self.order:
            for d in o.deps:
                d.sig = True
        for e in COMPUTE:
            c = 0
            for o in self.ops[e]:
                if o.is_dma:
                    continue
                if o.sig:
                    c += 1
                    o.count = c
        prog = self

        def run(eng_name, e, final_wait=False):
            waited = {}
            for o in prog.ops[eng_name]:
                for d in o.deps:
                    if d.is_dma:
                        key, val, sem = d.sem, d.val, dsems[d.sem]
                    else:
                        key, val, sem = d.eng, d.count, sems[d.eng]
                    if waited.get(key, 0) >= val:
                        continue
                    waited[key] = val
                    e.wait_ge(sem, val)
                ins = o.fn(e)
                if o.is_dma:
                    ins.then_inc(dsems[o.sem], 16)
                elif o.sig:
                    ins.then_inc(sems[o.eng], 1)
            if final_wait:
                for d in prog.out_dmas:
                    if waited.get(d.sem, 0) >= d.val:
                        continue
                    waited[d.sem] = d.val
                    e.wait_ge(dsems[d.sem], d.val)

        return run


class Bump:
    def __init__(self, arena, base, limit):
        self.arena = arena
        self.base = base
        self.limit = limit
        self.off = base

    def reset(self):
        self.off = self.base

    def _take(self, nbytes):
        off = (self.off + 63) // 64 * 64
        assert off + nbytes <= self.limit, (off, nbytes, self.limit)
        self.off = off + nbytes
        return off

    def bf(self, n):
        off = self._take(2 * n)
        return self.arena[:, off // 2: off // 2 + n]

    def f32(self, n):
        off = self._take(4 * n)
        return self.arena[:, off // 2: off // 2 + 2 * n].bitcast(F32)


ARENA_BYTES = 207 * 1024
PERS_END = 86016
XREG_END = PERS_END + 50688


def build_nc(debug_phase=None):
    nc = bass.Bass("TRN2", target_bir_lowering=False)

    def din(name, shape):
        return nc.dram_tensor(name, list(shape), F32, kind="ExternalInput").ap()

    xs = din("xs", [NKEY, D])
    cT_d = din("cT", [128, 16])
    wmod_d = din("wmod", [D, 3 * D])
    bcol_d = din("bcol", [128, 24])
    bgate_d = din("bgate", [1, D])
    gT_d = din("gT", [128, 8])
    win_d = din("w_in", [D, 5024])
    wkrsw_d = din("wkrsw", [D, 32])
    gckv_d = din("gckv", [128, 1])
    gcq_d = din("gcq", [128, 2])
    wuq_d = din("wuq", [256, 1024])
    wukv_d = din("wukv", [128, 1024])
    tabC_d = din("tabC", [32, NKEY])
    tabS_d = din("tabS", [32, NKEY])
    nab_d = din("nab", [8, 128, NVAR * 128])
    woa_d = din("woa", [512, D])
    wob_d = din("wob", [512, D])
    wout_d = din("wout", [D, D])
    fgb_d = din("fgb", [128, D])
    ident_d = din("ident", [128, 128])
    out_d = nc.dram_tensor("out", [NOWN, D], F32, kind="ExternalOutput").ap()
    DBGN = 16384
    dbg_d = None
    if debug_phase is not None:
        dbg_d = nc.dram_tensor("dbg", [128, DBGN], F32, kind="ExternalOutput").ap()

    win_v = win_d.rearrange("(k p) n -> p k n", p=128)

    with ExitStack() as st:
        E = st.enter_context
        AR = E(nc.sbuf_tensor("arena", [128, ARENA_BYTES // 2], BF16))[:]
        PS2A = E(nc.psum_tensor("ps01", [128, 1024], F32))[:]
        PS2B = E(nc.psum_tensor("ps23", [128, 1024], F32))[:]
        PS2 = [PS2A, PS2B]
        PSB = [PS2A[:, 0:512], PS2A[:, 512:1024], PS2B[:, 0:512], PS2B[:, 512:1024]]
        PSB += [E(nc.psum_tensor(f"psb{i}", [128, 512], F32))[:] for i in range(4, 8)]
        sems = {e: E(nc.semaphore("s_" + e)) for e in COMPUTE}
        dsems = {(q, i): E(nc.semaphore(f"d_{q}_{i}")) for q in QUEUES for i in range(NDMASEM)}
        block = E(nc.Block())
        P = Prog()

        def psk(i):
            return "ps%d" % i

        def ps_bf(i):
            return PSB[i].bitcast(BF16)[:, 0:1024].rearrange("p (k t) -> p k t", k=8)

        pers = Bump(AR, 0, PERS_END)
        hT_keep = pers.bf(8 * NKEEP * 128).rearrange("p (k t) -> p k t", k=8)
        maT = pers.bf(4 * NOWN).rearrange("p (c t) -> p c t", c=4)
        mbT = pers.bf(4 * NOWN).rearrange("p (c t) -> p c t", c=4)
        gate_b = pers.f32(D)
        fg_b = pers.f32(D)
        identb = pers.bf(128)
        idf = pers.f32(128)
        ones128 = pers.f32(128)
        SC = pers.f32(16).rearrange("p (k j) -> p k j", j=2)
        SH = pers.f32(16).rearrange("p (k j) -> p k j", j=2)
        colmod = pers.f32(32).rearrange("p (k j) -> p k j", j=2)
        gT = pers.f32(8)
        gckv = pers.f32(1)
        gcq = pers.f32(2)
        mhalf = pers.f32(1)
        ssq = pers.f32(NTILE + 16)
        rsv = pers.f32(NTILE + 16)
        rstd = pers.f32(NTILE + 16)
        cT = pers.f32(16).rearrange("p (k j) -> p k j", j=2)
        sT = pers.f32(16).rearrange("p (k j) -> p k j", j=2)

        xreg = Bump(AR, PERS_END, XREG_END)
        ckvnT = xreg.bf(NKEY)
        cqnT = xreg.bf(2 * NOWN).rearrange("p (k t) -> p k t", k=2)
        Kt = [xreg.bf(NKEY), xreg.bf(NKEY)]
        tabC_own = xreg.f32(NOWN)
        tabS_own = xreg.f32(NOWN)

        NAW = [cqnT.rearrange("p k t -> p (k t)").rearrange("p (k n) -> p k n", k=8),
               tabC_own.bitcast(BF16).rearrange("p (k n) -> p k n", k=8),
               tabS_own.bitcast(BF16).rearrange("p (k n) -> p k n", k=8)]
        T = Bump(AR, XREG_END, ARENA_BYTES)
        TX = Bump(AR, PERS_END, ARENA_BYTES)

        dbg_items = []

        EPS_AP = pers.f32(1)
        bcol = pers.f32(24)
        colmod3 = pers.f32(48).rearrange("p (j t) -> p j t", t=2)
        TS_BYTES = 16384 + 3072 + 1024 + 2048 + 256
        TS = Bump(AR, ARENA_BYTES - TS_BYTES, ARENA_BYTES)
        T.limit = ARENA_BYTES - TS_BYTES
        wz = [TS.bf(8 * 512).rearrange("p (k n) -> p k n", k=8) for _ in range(2)]
        wuq = TS.bf(2 * 1024).rearrange("p (k n) -> p k n", k=2)
        wukv = TS.bf(1024)
        P.dma("sp", idf, ident_d[:, :], writes=["idf"])
        P.dma("sp", cT.rearrange("p k j -> p (k j)"), cT_d[:, :], writes=["cT"])
        P.dma("sp", gT, gT_d[:, :], writes=["gT"])
        P.dma("sp", bcol, bcol_d[:, :], writes=["bcol"])
        P.dma("sp", gckv, gckv_d[:, :], writes=["gckv"])
        P.dma("sp", gcq, gcq_d[:, :], writes=["gcq"])
        P.dma("sp", fg_b, fgb_d[:, :], writes=["fg_b"])
        P.op("dve", lambda e: e.tensor_copy(out=identb, in_=idf), reads=["idf"], writes=["identb"])
        P.op("pool", lambda e: e.memset(ones128, 1.0), writes=["ones128"])
        P.op("pool", lambda e: e.memset(mhalf, -0.5), writes=["mhalf"])
        P.op("pool", lambda e: e.memset(EPS_AP, EPS), writes=["epsap"])
        P.op("act", lambda e: e.activation(out=sT.rearrange("p k j -> p (k j)"), in_=cT.rearrange("p k j -> p (k j)"), func=AF.Silu),
             reads=["cT"], writes=["sT"])
        WMB = Bump(AR, 40960, 40960 + 32768)
        wm = [WMB.f32(2 * D) for _ in range(3)]
        modrow = WMB.f32(2 * D)
        WM_KEYS = [("wm", i) for i in range(3)] + [("modrow", n) for n in range(16)]
        wmod_v = wmod_d.rearrange("(k p) n -> p k n", p=128)

        XT0 = Bump(AR, XREG_END, ARENA_BYTES)
        xt_pre = [XT0.f32(D) for _ in range(4)]
        for k in range(8):
            buf = wm[k % 3]
            bk = ("wm", k % 3)
            P.dma("sp", buf, wmod_d[k * 128:(k + 1) * 128, 0:2 * D], writes=[bk])
            if k == 2:
                P.dma("sp", xt_pre[0], xs[0:128, :], writes=[("xt", 0)])
                P.dma("sp", xt_pre[1], xs[128:256, :], writes=[("xt", 1)])
            for n in range(4):
                P.op("pe", (lambda e, k=k, n=n, buf=buf: e.matmul(PSB[2 + n][0:2, :], lhsT=sT[:, k, :], rhs=buf[:, n * 512:(n + 1) * 512], start=(k == 0), stop=(k == 7), skip_group_check=True)),
                     reads=["sT", bk], writes=[psk(2 + n)])
        for n in range(4):
            P.op("dve", (lambda e, n=n: e.tensor_copy(out=modrow[0:2, n * 512:(n + 1) * 512], in_=PSB[2 + n][0:2, :])),
                 reads=[psk(2 + n)], writes=[("modrow", 4 * n + i_) for i_ in range(4)])
        pcm = PSB[4][:, 0:32].rearrange("p (j t) -> p j t", t=2)
        for j in range(16):
            P.op("pe", (lambda e, j=j: e.matmul(pcm[:, j, :], lhsT=modrow[0:2, j * 128:(j + 1) * 128], rhs=idf[0:2, 0:2], start=True, stop=True, skip_group_check=True)),
                 reads=[("modrow", j), "idf"], writes=[psk(4)])
        P.op("dve", lambda e: e.tensor_tensor(out=colmod3[:, 0:16, :], in0=pcm, in1=bcol[:, 0:16].unsqueeze(2).broadcast_to([128, 16, 2]), op=ALU.add),
             reads=[psk(4), "bcol"], writes=["colmod"])
        P.op("dve", lambda e: e.tensor_copy(out=SH, in_=colmod3[:, 0:8, :]), reads=["colmod"], writes=["SH"])
        P.op("dve", lambda e: e.scalar_tensor_tensor(out=SC, in0=colmod3[:, 8:16, :], scalar=1.0, in1=gT.unsqueeze(2).broadcast_to([128, 8, 2]), op0=ALU.add, op1=ALU.mult),
             reads=["colmod", "gT"], writes=["SC"])

        if debug_phase == 0:
            dbg_items.append((SC.rearrange("p k j -> p (k j)"), 16, ["SC"]))
            dbg_items.append((SH.rearrange("p k j -> p (k j)"), 16, ["SH"]))

        ZBLK = 4

        def phase1():
            T.reset()
            xt = [T.f32(D) for _ in range(4)]
            junk = T.bf(D)
            xn = [T.bf(D) for _ in range(2)]
            hT_rot = T.bf(8 * 512).rearrange("p (k t) -> p k t", k=8)
            tabC_r = [T.f32(512) for _ in range(1)]
            tabS_r = [T.f32(512) for _ in range(1)]
            wkv = T.bf(8 * 192).rearrange("p (k n) -> p k n", k=8)
            wcq = T.bf(8 * 256).rearrange("p (k n) -> p k n", k=8)
            sq = [T.bf(512) for _ in range(3)]
            onesb = T.bf(128)
            rstd_b = T.f32(512)
            lnv = rstd_b
            t1 = T.f32(512)
            t2 = T.f32(512)
            P.dma("pool", wkv[:, :, 0:160], win_v[:, :, 0:160], writes=["wkv_a"])
            P.dma("pool", wkv[:, :, 160:192], wkrsw_d.rearrange("(k p) n -> p k n", p=128), writes=["wkv_b"])
            P.dma("pool", wcq, win_v[:, :, 1184:1440], writes=["wcq"])
            P.op("pool", lambda e: e.memset(onesb, 1.0), writes=["onesb"])
            nrest = [0]

            def stage_c(s0, ntl):
                N = ntl * 128
                c0 = s0 * 128
                keep = s0 < NKEEP
                if keep:
                    hsrc = lambda k: hT_keep[:, k, c0:c0 + N]
                    hkeys = [("hk", s) for s in range(s0, s0 + ntl)]
                else:
                    hsrc = lambda k: hT_rot[:, k, 0:N]
                    hkeys = ["hrot"]
                own = c0 < NOWN
                if own:
                    tC = tabC_own[64:96, c0:c0 + N]
                    tS = tabS_own[64:96, c0:c0 + N]
                    tkeys = ["tabC_own", "tabS_own"]
                else:
                    tC = tabC_r[0][64:96, 0:N]
                    tS = tabS_r[0][64:96, 0:N]
                    tkeys = [("tabr", 0)]

                def mm_ckv():
                    for k in range(8):
                        P.op("pe", (lambda e, k=k: e.matmul(PSB[2][:, 0:N], lhsT=wkv[:, k, 0:128], rhs=hsrc(k), start=(k == 0), stop=(k == 7))),
                             reads=["wkv_a"] + hkeys, writes=[psk(2)])

                def mm_kr():
                    for k in range(8):
                        P.op("pe", (lambda e, k=k: e.matmul(PSB[3][64:128, 0:N], lhsT=wkv[:, k, 128:192], rhs=hsrc(k), start=(k == 0), stop=(k == 7))),
                             reads=["wkv_a", "wkv_b"] + hkeys, writes=[psk(3)])

                def mm_ksw():
                    pass

                def p1():
                    if not own:
                        P.dma("sp", tC, tabC_d[:, c0:c0 + N], writes=tkeys)
                        P.dma("sp", tS, tabS_d[:, c0:c0 + N], writes=tkeys)
                    mm_ckv()
                    if not keep:
                        mm_kr()
                        mm_ksw()

                def p2():
                    if keep:
                        mm_kr()
                    P.op("act", lambda e: e.activation(out=sq[0][:, 0:N], in_=PSB[2][:, 0:N], func=AF.Square), reads=[psk(2)], writes=["sq0"])

                def p3():
                    if keep:
                        mm_ksw()
                    P.op("pe", lambda e: e.matmul(PSB[5][:, 0:N], lhsT=onesb, rhs=sq[0][:, 0:N], start=True, stop=True),
                         reads=["onesb", "sq0"], writes=[psk(5)])

                def p4():
                    P.op("act", lambda e: e.activation(out=lnv[:, 0:N], in_=PSB[5][:, 0:N], func=AF.Ln, scale=1.0 / 128, bias=EPS_AP), reads=[psk(5), "epsap"], writes=["rstd_b"])
                    P.op("act", lambda e: e.activation(out=rstd_b[:, 0:N], in_=lnv[:, 0:N], func=AF.Exp, scale=-0.5), reads=["rstd_b"], writes=["rstd_b"])
                    P.op("dve", lambda e: e.tensor_tensor(out=t1[64:96, 0:N], in0=PSB[3][64:96, 0:N], in1=tC, op=ALU.mult), reads=[psk(3)] + tkeys, writes=["t1"])
                    P.op("dve", lambda e: e.tensor_tensor(out=t2[64:96, 0:N], in0=PSB[3][96:128, 0:N], in1=tS, op=ALU.mult), reads=[psk(3)] + tkeys, writes=["t2"])
                    P.op("dve", lambda e: e.tensor_tensor(out=Kt[0][64:96, c0:c0 + N], in0=t1[64:96, 0:N], in1=t2[64:96, 0:N], op=ALU.add), reads=["t1", "t2"], writes=[("ktr", 0, s0)])
                    if own:
                        for k2 in range(2):
                            for k in range(8):
                                P.op("pe", (lambda e, k=k, k2=k2: e.matmul(PSB[6 + k2][:, 0:N], lhsT=wcq[:, k, k2 * 128:(k2 + 1) * 128], rhs=hsrc(k), start=(k == 0), stop=(k == 7))),
                                     reads=["wcq"] + hkeys, writes=[psk(6 + k2)])

                def p5():
                    P.op("dve", lambda e: e.scalar_tensor_tensor(out=ckvnT[:, c0:c0 + N], in0=PSB[2][:, 0:N], scalar=gckv[:, 0:1], in1=rstd_b[:, 0:N], op0=ALU.mult, op1=ALU.mult),
                         reads=[psk(2), "gckv", "rstd_b"], writes=[("ckvn", s0)])
                    P.op("pool", lambda e: e.tensor_copy(out=Kt[1][64:96, c0:c0 + N], in_=Kt[0][64:96, c0:c0 + N]), reads=[("ktr", 0, s0)], writes=[("ktr", 1, s0)])
                    if own:
                        for k2 in range(2):
                            P.op("act", (lambda e, k2=k2: e.activation(out=sq[1 + k2][:, 0:N], in_=PSB[6 + k2][:, 0:N], func=AF.Square)), reads=[psk(6 + k2)], writes=["sq%d" % (1 + k2)])

                def p6():
                    if own:
                        for k2 in range(2):
                            P.op("pe", (lambda e, k2=k2: e.matmul(PSB[5][:, 0:N], lhsT=onesb, rhs=sq[1 + k2][:, 0:N], start=(k2 == 0), stop=(k2 == 1))),
                                 reads=["onesb", "sq%d" % (1 + k2)], writes=[psk(5)])

                def p7():
                    if own:
                        P.op("act", lambda e: e.activation(out=lnv[:, 0:N], in_=PSB[5][:, 0:N], func=AF.Ln, scale=1.0 / 256, bias=EPS_AP), reads=[psk(5), "epsap"], writes=["rstd_b"])
                        P.op("act", lambda e: e.activation(out=rstd_b[:, 0:N], in_=lnv[:, 0:N], func=AF.Exp, scale=-0.5), reads=["rstd_b"], writes=["rstd_b"])

                def p8():
                    if own:
                        for k2 in range(2):
                            P.op("dve", (lambda e, k2=k2: e.scalar_tensor_tensor(out=cqnT[:, k2, c0:c0 + N], in0=PSB[6 + k2][:, 0:N], scalar=gcq[:, k2:k2 + 1], in1=rstd_b[:, 0:N], op0=ALU.mult, op1=ALU.mult)),
                                 reads=[psk(6 + k2), "gcq", "rstd_b"], writes=[("cqn", s0)])

                return [p1, p2, p3, p4, p5, p6, p7, p8]

            def load_x(s):
                P.dma("sp", xt[s % 4], xs[s * 128:(s + 1) * 128, :], writes=[("xt", s % 4)])

            def t_square(s):
                xb = xt[s % 4]
                xk = ("xt", s % 4)
                P.op("act", (lambda e: e.activation(out=junk, in_=xb, func=AF.Square, accum_out=ssq[:, s:s + 1])),
                     reads=[xk], writes=["junk", ("ssq", s)])
                P.op("pool", (lambda e: e.tensor_scalar(out=rsv[:, s:s + 1], in0=ssq[:, s:s + 1], scalar1=1.0 / D, scalar2=EPS, op0=ALU.mult, op1=ALU.add)),
                     reads=[("ssq", s)], writes=[("rsv", s)])
                P.op("pool", (lambda e: e.tensor_tensor(out=rstd[:, s:s + 1], in0=rsv[:, s:s + 1], in1=mhalf[:, 0:1], op=ALU.pow)),
                     reads=[("rsv", s), "mhalf"], writes=[("rstd", s)])

            def t_norm(s):
                xb = xt[s % 4]
                P.op("dve", (lambda e: e.tensor_scalar(out=xn[s % 2], in0=xb, scalar1=rstd[:, s:s + 1], scalar2=None, op0=ALU.mult)),
                     reads=[("xt", s % 4), ("rstd", s)], writes=[("xn", s % 2)])

            def t_transpose(s):
                xnb = xn[s % 2]
                pt = ps_bf(s % 2)
                for k in range(8):
                    P.op("pe", (lambda e, k=k: e.transpose(pt[:, k, :], xnb[:, k * 128:(k + 1) * 128], identb)),
                         reads=[("xn", s % 2), "identb"], writes=[psk(s % 2)])

            deferred = []

            def t_evac(s, it):
                pb = s % 2
                pt = ps_bf(pb)
                j = 1 if s in (18, 19) else 0
                for k in range(8):
                    if s < NKEEP:
                        dst = hT_keep[:, k, s * 128:(s + 1) * 128]
                        wkey = ("hk", s)
                    else:
                        r = (s - NKEEP) % 4
                        dst = hT_rot[:, k, r * 128:(r + 1) * 128]
                        wkey = "hrot"
                    if k < 5:
                        P.op("dve", (lambda e, k=k, dst=dst: e.tensor_scalar(out=dst, in0=pt[:, k, :], scalar1=SC[:, k, j:j + 1], scalar2=SH[:, k, j:j + 1], op0=ALU.mult, op1=ALU.add)),
                             reads=[psk(pb), "SC", "SH"], writes=[wkey])
                    else:
                        P.op("act", (lambda e, k=k, dst=dst: e.activation(out=dst, in_=pt[:, k, :], func=AF.Identity, scale=SC[:, k, j:j + 1], bias=SH[:, k, j:j + 1])),
                             reads=[psk(pb), "SC", "SH"], writes=[wkey])
                parts = None
                if s % 4 == 3:
                    parts = stage_c(s - 3, 4)
                elif s == NTILE - 1:
                    parts = stage_c(s - 1, 2)
                if parts:
                    lag = 2 if s == NTILE - 1 else 0
                    for d_, p_ in enumerate(parts):
                        deferred.append((it + 1 + lag + d_, p_))

            P.dma("sp", tabC_own[64:96, :], tabC_d[:, 0:NOWN], writes=["tabC_own"])
            P.dma("sp", tabS_own[64:96, :], tabS_d[:, 0:NOWN], writes=["tabS_own"])
            P.dma("pool", wz[0], win_v[:, :, 1440:1952], writes=["wz0"])
            P.dma("pool", wz[1], win_v[:, :, 2464:2976], writes=["wz1"])
            prefetch = [lambda: P.dma("pool", wuq, wuq_d.rearrange("(k p) n -> p k n", p=128), writes=["wuq"]),
                        lambda: P.dma("pool", wukv, wukv_d[:, :], writes=["wukv"])]
            zq = [(which, c, blk) for blk in range(ZBLK) for which in range(2) for c in range(4)]
            zcount = [0]

            def z_mm(which, c, blk):
                for k in range(8):
                    P.op("pe", (lambda e, k=k: e.matmul(PSB[4], lhsT=wz[which][:, k, c * 128:(c + 1) * 128], rhs=hT_keep[:, k, blk * 512:(blk + 1) * 512], start=(k == 0), stop=(k == 7))),
                         reads=["wz%d" % which] + [("hk", s) for s in range(blk * 4, blk * 4 + 4)], writes=[psk(4)])

            def z_evac(which, c, blk):
                dstT = maT if which == 0 else mbT
                n_ = zcount[0]
                zcount[0] += 1
                if n_ % 2 == 0:
                    P.op("act", (lambda e: e.activation(out=dstT[:, c, blk * 512:(blk + 1) * 512], in_=PSB[4], func=AF.Copy)),
                         reads=[psk(4)], writes=[("mT", which, c, blk)] + WM_KEYS)
                else:
                    P.op("dve", (lambda e: e.tensor_copy(out=dstT[:, c, blk * 512:(blk + 1) * 512], in_=PSB[4])),
                         reads=[psk(4)], writes=[("mT", which, c, blk)] + WM_KEYS)

            for it in range(NTILE + 3):
                deferred.sort(key=lambda t_: t_[0])
                while deferred and deferred[0][0] <= it:
                    deferred.pop(0)[1]()
                if zq and it >= 4 * zq[0][2] + 8:
                    g_ = zq.pop(0)
                    z_mm(*g_)
                    deferred.append((it + 1, (lambda g_=g_: z_evac(*g_))))
                if it < NTILE:
                    t_square(it)
                if 0 <= it - 1 < NTILE:
                    t_norm(it - 1)
                if it + 2 < NTILE:
                    load_x(it + 2)
                if 0 <= it - 2 < NTILE:
                    t_transpose(it - 2)
                if 0 <= it - 3 < NTILE:
                    t_evac(it - 3, it)
                if it >= 8 and it % 4 == 0 and prefetch:
                    prefetch.pop(0)()
            it = NTILE + 3
            while zq or deferred:
                deferred.sort(key=lambda t_: t_[0])
                while deferred and deferred[0][0] <= it:
                    deferred.pop(0)[1]()
                if zq:
                    g_ = zq.pop(0)
                    z_mm(*g_)
                    deferred.append((it + 1, (lambda g_=g_: z_evac(*g_))))
                it += 1
            while prefetch:
                prefetch.pop(0)()

        if debug_phase is None or debug_phase >= 1:
            phase1()
        if debug_phase == 1:
            allk = [("ckvn", s) for s in range(0, NTILE, 4)]
            dbg_items.append((ckvnT, NKEY, allk))
            dbg_items.append((Kt[0], NKEY, [("ktr", 0, s) for s in range(0, NTILE, 4)]))
            dbg_items.append((Kt[1], NKEY, [("ktr", 1, s) for s in range(0, NTILE, 4)]))
            dbg_items.append((cqnT[:, 0, :], NOWN, [("cqn", s) for s in range(0, 16, 4)]))

        def phase2_groups():
            groups = []
            i = 0
            for which in range(2):
                dstT = maT if which == 0 else mbT
                for c in range(4):
                    for blk in range(ZBLK, 4):
                        pb = 4 + i % 2
                        i += 1

                        def grp(which=which, dstT=dstT, c=c, blk=blk, pb=pb):
                            for k in range(8):
                                P.op("pe", (lambda e, k=k: e.matmul(PSB[pb], lhsT=wz[which][:, k, c * 128:(c + 1) * 128], rhs=hT_keep[:, k, blk * 512:(blk + 1) * 512], start=(k == 0), stop=(k == 7))),
                                     reads=["wz%d" % which] + [("hk", s) for s in range(blk * 4, blk * 4 + 4)], writes=[psk(pb)])
                            P.op("act", (lambda e: e.activation(out=dstT[:, c, blk * 512:(blk + 1) * 512], in_=PSB[pb], func=AF.Silu)),
                                 reads=[psk(pb)], writes=[("mT", which, c, blk)] + WM_KEYS)
                        groups.append(grp)
            return groups

        NPRE = 0
        p2groups = phase2_groups() if (debug_phase is None or debug_phase >= 2) else []

        XA = Bump(AR, PERS_END, PERS_END + 8704)
        XB = Bump(AR, PERS_END + 8704 + 8192, PERS_END + 8704 + 8192 + 17408)
        qbT2 = [XA.bf(2048), XA.bf(2048)]
        kbT2 = [XB.bf(NKEEP * 128) for _ in range(2)]
        Bt0 = XB.bf(NVAR * 128)
        NAWQ, NAWK = NAW[1], NAW[2]

        def pair_chunks(hp, bank_fn, extra_q=(), extra_k=()):
            pb_ = hp % 2
            chunks = []
            st_ = {}
            for which, nblk in (("q", 4), ("k", 5)):
                for blk in range(nblk):
                    for k0 in range(0, 8, 2):
                        def ch(which=which, blk=blk, k0=k0):
                            if k0 == 0:
                                st_["pb"] = bank_fn()
                            pb = st_["pb"]
                            w_ = NAWQ if which == "q" else NAWK
                            wkey = "wqb" if which == "q" else "wkb"
                            for k in (k0, k0 + 1):
                                P.op("pe", (lambda e, k=k: e.matmul(PSB[pb], lhsT=w_[:, k, hp * 128:(hp + 1) * 128], rhs=hT_keep[:, k, blk * 512:(blk + 1) * 512], start=(k == 0), stop=(k == 7))),
                                     reads=[wkey] + [("hk", s) for s in range(blk * 4, blk * 4 + 4)], writes=[psk(pb)])
                            if k0 == 6:
                                if which == "q":
                                    P.op("dve", (lambda e: e.tensor_scalar(out=qbT2[pb_][:, blk * 512:(blk + 1) * 512], in0=PSB[pb], scalar1=NA_SCALE, scalar2=None, op0=ALU.mult)),
                                         reads=[psk(pb)], writes=[("qbT", pb_)] + list(extra_q))
                                else:
                                    P.op("dve", (lambda e: e.tensor_copy(out=kbT2[pb_][:, blk * 512:(blk + 1) * 512], in_=PSB[pb])),
                                         reads=[psk(pb)], writes=[("kbT", pb_)] + list(extra_k))
                        chunks.append(ch)
            return chunks

        def phase3():
            for _ in range(NPRE):
                p2groups.pop(0)()
            T.reset()
            Vext4 = [[T.bf(NTILE * 128).rearrange("p (t c) -> p t c", c=128) for _ in range(2)] for _ in range(2)]
            vbuf = lambda hh: Vext4[hh % 2][(hh // 2) % 2]
            vkey = lambda hh: (hh % 2) * 2 + (hh // 2) % 2
            wukv4 = wukv.rearrange("p (h two c) -> p h two c", two=2, c=64)
            wz0_flat = wz[0].rearrange("p k n -> p (k n)")
            Qt = [wz0_flat[:, 0:NOWN], wz0_flat[:, NOWN:2 * NOWN]]
            PT = [T.bf(1024) for _ in range(3)]
            wz1_flat = wz[1].rearrange("p k n -> p (k n)")
            t1 = wz1_flat[:, 0:1024].bitcast(F32)
            t2 = wz1_flat[:, 1024:2048].bitcast(F32)
            recip = [T.f32(512) for _ in range(2)]
            tn = [T.f32(512) for _ in range(2)]
            BB = [6, 7]
            bbi = [0]

            def nextbb():
                b = BB[bbi[0] % len(BB)]
                bbi[0] += 1
                return b

            def build_chunks(h):
                hb = h % 2
                voff = 0 if hb == 0 else 64
                chunks = []

                def k_chunk(blk):
                    c0 = blk * 512
                    N = min(512, NKEY - c0)
                    pb = nextbb()
                    P.op("pe", (lambda e: e.matmul(PSB[pb][0:64, 0:N], lhsT=wukv[:, h * 128:h * 128 + 64], rhs=ckvnT[:, c0:c0 + N], start=True, stop=True)),
                         reads=["wukv", ("ckvn", blk * 4)], writes=[psk(pb)])
                    P.op("dve", (lambda e: e.tensor_copy(out=Kt[hb][0:64, c0:c0 + N], in_=PSB[pb][0:64, 0:N])),
                         reads=[psk(pb)], writes=[("ktn", hb)])

                for blk in range(9):
                    chunks.append(lambda blk=blk: k_chunk(blk))

                def v_group(g0, ng, heads):
                    nh = len(heads)
                    pb = nextbb()
                    pv = PSB[pb][:, 0:ng * nh * 64].rearrange("p (t a c) -> p t a c", a=nh, c=64)
                    for i in range(ng):
                        tl = g0 + i
                        P.op("pe", (lambda e, i=i, tl=tl: e.matmul(pv[:, i, :, :], lhsT=ckvnT[:, tl * 128:(tl + 1) * 128], rhs=wukv4[:, heads[0]:heads[0] + nh, 1, :], start=True, stop=True, skip_group_check=True)),
                             reads=["wukv", ("ckvn", (tl // 4) * 4 if tl < 32 else 32)], writes=[psk(pb)])
                    for a_, hh in enumerate(heads):
                        par = hh % 2
                        P.op("dve", (lambda e, a_=a_, hh=hh, par=par: e.tensor_copy(out=vbuf(hh)[:, g0:g0 + ng, par * 64:par * 64 + 64], in_=pv[:, 0:ng, a_, :])),
                             reads=[psk(pb)], writes=[("vextv", vkey(hh))])

                if h in (0, 7):
                    vheads = [h]
                elif h % 2 == 1:
                    vheads = [h, h + 1]
                else:
                    vheads = []
                if vheads:
                    gsz = 8 // len(vheads)
                    for g0 in range(0, NTILE, gsz):
                        ng = min(gsz, NTILE - g0)
                        for i0 in range(0, ng, 2):
                            pass
                        chunks.append(lambda g0=g0, ng=ng: v_group(g0, ng, vheads))

                def q_chunk(blk):
                    c0 = blk * 512
                    pa = nextbb()
                    for k in range(2):
                        P.op("pe", (lambda e, k=k: e.matmul(PSB[pa], lhsT=wuq[:, k, h * 128:(h + 1) * 128], rhs=cqnT[:, k, c0:c0 + 512], start=(k == 0), stop=(k == 1))),
                             reads=["wuq", ("cqn", blk * 4)], writes=[psk(pa)])
                    P.op("dve", (lambda e: e.tensor_copy(out=Qt[hb][0:64, c0:c0 + 512], in_=PSB[pa][0:64, :])),
                         reads=[psk(pa)], writes=[("qt", hb), "wz0"])
                    P.op("dve", (lambda e: e.tensor_tensor(out=t1[64:96, :], in0=PSB[pa][64:96, :], in1=tabC_own[64:96, c0:c0 + 512], op=ALU.mult)),
                         reads=[psk(pa), "tabC_own"], writes=["t1", "wz1"])
                    P.op("dve", (lambda e: e.tensor_tensor(out=t2[64:96, :], in0=PSB[pa][96:128, :], in1=tabS_own[64:96, c0:c0 + 512], op=ALU.mult)),
                         reads=[psk(pa), "tabS_own"], writes=["t2", "wz1"])
                    P.op("dve", (lambda e: e.tensor_tensor(out=Qt[hb][64:96, c0:c0 + 512], in0=t1[64:96, :], in1=t2[64:96, :], op=ALU.add)),
                         reads=["t1", "t2"], writes=[("qt", hb), "wz0"])

                for blk in range(4):
                    chunks.append(lambda blk=blk: q_chunk(blk))
                return chunks

            steps = [(h, qb, p) for h in range(8) for qb in range(4) for p in range(17)]

            def s_mm(i):
                h, qb, p = steps[i]
                hb = h % 2
                sp_ = i % 2
                for half in range(2):
                    tl = 2 * p + half
                    bank = 2 * sp_ + half
                    P.op("pe", (lambda e, tl=tl, bank=bank, hb=hb, qb=qb: e.matmul(PSB[bank], lhsT=Kt[hb][0:96, tl * 128:(tl + 1) * 128], rhs=Qt[hb][0:96, qb * 512:(qb + 1) * 512], start=True, stop=True)),
                         reads=[("ktn", hb), ("ktr", hb, (tl // 4) * 4 if tl < 32 else 32), ("qt", hb)], writes=[("spair", sp_)])

            norm_pending = []

            def norm_chunks(h, qb, ob):
                hb = h % 2
                orow = slice(0, 64) if hb == 0 else slice(64, 128)
                lrow = slice(64, 128) if hb == 0 else slice(0, 64)
                g = (h * 4 + qb) % 2
                c = h // 2
                out = []
                for q4 in range(4):
                    cs = slice(q4 * 128, (q4 + 1) * 128)
                    out.append(lambda cs=cs, q4=q4: P.op("dve", (lambda e: e.reciprocal(out=recip[g][lrow, cs], in_=PSB[ob][lrow, cs])),
                                                         reads=[psk(ob)], writes=[("recip", g, q4)]))
                out.append(lambda: P.op("dve", (lambda e: e.tensor_tensor(out=tn[g][orow, :], in0=PSB[ob][orow, :], in1=recip[g][lrow, :], op=ALU.mult)),
                                        reads=[psk(ob)] + [("recip", g, q4) for q4 in range(4)], writes=[("tn", g)]))
                out.append(lambda: P.op("pool", (lambda e: e.tensor_tensor(out=maT[orow, c, qb * 512:(qb + 1) * 512], in0=tn[g][orow, :], in1=maT[orow, c, qb * 512:(qb + 1) * 512], op=ALU.mult)),
                                        reads=[("tn", g), ("mT", 0, c, qb)], writes=[("mT", 0, c, qb)]))
                return out

            def emit_pv(i):
                h, qb, p = steps[i]
                hb = h % 2
                pt = PT[i % 3]
                ptk = ("pt", i % 3)
                ob = 4 + (h * 4 + qb) % 2
                for half in range(2):
                    tl = 2 * p + half
                    P.op("pe", (lambda e, tl=tl, half=half: e.matmul(PSB[ob], lhsT=vbuf(h)[:, tl, :], rhs=pt[:, half * 512:(half + 1) * 512], start=(tl == 0), stop=(tl == NTILE - 1))),
                         reads=[ptk, ("vextv", vkey(h)), ("vext1", vkey(h))], writes=[psk(ob)])
                if p == 16:
                    norm_pending.extend(norm_chunks(h, qb, ob))

            for which in range(2):
                dstT = maT if which == 0 else mbT
                for c in range(4):
                    P.op("act", (lambda e, dstT=dstT, c=c: e.activation(out=dstT[:, c, 0:ZBLK * 512], in_=dstT[:, c, 0:ZBLK * 512], func=AF.Silu)),
                         reads=[("mT", which, c, blk) for blk in range(ZBLK)], writes=[("mT", which, c, blk) for blk in range(ZBLK)])
            b0 = build_chunks(0)
            for _ in range(9):
                b0.pop(0)()
            for _ in range(4):
                b0.pop(-4 + _)()
            P.barrier()
            for s_ in range(2):
                P.op("pool", (lambda e, s_=s_: e.memset(Vext4[0][s_][:, :, 64:128], 1.0)), writes=[("vext1", s_)])
                P.op("pool", (lambda e, s_=s_: e.memset(Vext4[1][s_][:, :, 0:64], 1.0)), writes=[("vext1", 2 + s_)])
            BB[:] = [6, 7, 4, 5]
            bbi[0] = 0
            for grp in p2groups:
                grp()
                for _ in range(4):
                    if b0:
                        b0.pop(0)()
            while b0:
                b0.pop(0)()
            BB[:] = [6, 7]
            bbi[0] = 0
            pending = []
            s_mm(0)
            for i, (h, qb, p) in enumerate(steps):
                hb = h % 2
                sp_ = i % 2
                if qb == 0 and p == 0 and h + 1 < 8:
                    pending = build_chunks(h + 1)
                if qb == 1 and p == 0 and h == 7:
                    pending = pair_chunks(0, nextbb,
                                          extra_q=[("ckvn", s) for s in range(0, NTILE, 4)],
                                          extra_k=[("ktn", 0)] + [("ktr", 0, s) for s in range(0, NTILE, 4)])
                if qb == 0 and p == 0 and h == 7:
                    P.dma("pool", NAW[0], win_v[:, :, 672:1184], writes=["wvb"] + [("cqn", s) for s in (0, 4, 8, 12)])
                    P.dma("pool", NAW[1], win_v[:, :, 1952:2464], writes=["wqb", "tabC_own"])
                    P.dma("pool", NAW[2], win_v[:, :, 160:672], writes=["wkb", "tabS_own"])
                if i + 1 < len(steps):
                    s_mm(i + 1)
                if i >= 1:
                    emit_pv(i - 1)
                if pending and (len(pending) > (67 - (qb * 17 + p)) // 2 or p % 2 == 0):
                    pending.pop(0)()
                pt = PT[i % 3]
                ptk = ("pt", i % 3)
                P.op("act", (lambda e, sp_=sp_, pt=pt: e.activation(out=pt, in_=PS2[sp_], func=AF.Exp, scale=MLA_SCALE)),
                     reads=[("spair", sp_)], writes=[ptk])
                if norm_pending:
                    norm_pending.pop(0)()
            emit_pv(len(steps) - 1)
            assert not pending
            while norm_pending:
                norm_pending.pop(0)()

        if debug_phase is None or debug_phase >= 3:
            phase3()
        if debug_phase == 3:
            for c in range(4):
                dbg_items.append((maT[:, c, 0:1024], 1024, [("mT", 0, c, 0), ("mT", 0, c, 1)]))

        W5 = Bump(AR, ARENA_BYTES - 12288, ARENA_BYTES)
        wgs = []
        for _ in range(2):
            wgs.append(dict(ga=W5.bf(8 * 128).rearrange("p (k n) -> p k n", k=8), gb=W5.bf(8 * 128).rearrange("p (k n) -> p k n", k=8),
                            oa=W5.bf(4 * 128).rearrange("p (k n) -> p k n", k=4), ob=W5.bf(4 * 128).rearrange("p (k n) -> p k n", k=4)))
        woa_v = woa_d.rearrange("(k p) n -> p k n", p=128)
        wob_v = wob_d.rearrange("(k p) n -> p k n", p=128)

        def load_w(c):
            wb = wgs[c % 2]
            wk = ("wg", c % 2)
            P.dma("pool", wb["ga"], win_v[:, :, 2976 + c * 128:2976 + (c + 1) * 128], writes=[(wk, "ga")])
            P.dma("pool", wb["gb"], win_v[:, :, 4000 + c * 128:4000 + (c + 1) * 128], writes=[(wk, "gb")])
            P.dma("pool", wb["oa"], woa_v[:, :, c * 128:(c + 1) * 128], writes=[(wk, "oa")])
            P.dma("pool", wb["ob"], wob_v[:, :, c * 128:(c + 1) * 128], writes=[(wk, "ob")])

        def phase4():
            P.barrier()
            TX.reset()
            TX.off = XREG_END
            TX.limit = ARENA_BYTES - 12288
            Bt = [Bt0, TX.bf(NVAR * 128)]
            vbext = TX.bf(NKEEP * 8 * 128).rearrange("p (t hp two c) -> p t hp two c", t=NKEEP, hp=4, two=2, c=128)
            wvb, wqb, wkb = NAW
            PTn = [TX.bf(896) for _ in range(3)]
            recip = [TX.f32(512) for _ in range(2)]
            tn = [TX.f32(512) for _ in range(2)]
            P.dma("pool", Bt[0], nab_d[0], writes=[("bt", 0)])
            bbi = [0]

            def nextbb():
                b_ = 6 + bbi[0] % 2
                bbi[0] += 1
                return b_

            def build_v(t):
                P.op("pool", (lambda e: e.memset(vbext[:, t, :, 0, 64:128], 1.0)), writes=[("vb1", t)])
                P.op("pool", (lambda e: e.memset(vbext[:, t, :, 1, 0:64], 1.0)), writes=[("vb1", t)])
                pb = nextbb()
                for k in range(8):
                    P.op("pe", (lambda e, k=k: e.matmul(PSB[pb], lhsT=hT_keep[:, k, t * 128:(t + 1) * 128], rhs=wvb[:, k, :], start=(k == 0), stop=(k == 7))),
                         reads=["wvb", ("hk", t)], writes=[psk(pb)])
                pv = PSB[pb].rearrange("p (hp two c) -> p hp two c", hp=4, two=2, c=64)
                P.op("dve", (lambda e: e.tensor_copy(out=vbext[:, t, :, 0, 0:64], in_=pv[:, :, 0, :])), reads=[psk(pb)], writes=[("vbv", t)])
                P.op("dve", (lambda e: e.tensor_copy(out=vbext[:, t, :, 1, 64:128], in_=pv[:, :, 1, :])), reads=[psk(pb)], writes=[("vbv", t)])

            def build_pair(hp):
                for ch in pair_chunks(hp, nextbb):
                    ch()

            steps = [(h, jl) for h in range(8) for jl in range(16)]

            def slots_of(jl):
                kts = [0, 1, 2, 3] if jl < 2 else list(range(jl - 2, jl + 3))
                v0 = 0 if jl == 0 else (4 if jl == 1 else 8)
                return kts, v0

            def s_mm(i):
                h, jl = steps[i]
                hb = h % 2
                pb_ = (h // 2) % 2
                rs = slice(64 * (h % 2), 64 * (h % 2) + 64)
                sreg = PS2[i % 2]
                sk = ("spair", i % 2)
                kts, v0 = slots_of(jl)
                nw = len(kts)
                rdb = [("bt", hb), "identb"]
                rdq = [("qbT", pb_), ("kbT", pb_)]
                qv = qbT2[pb_][rs, jl * 128:(jl + 1) * 128]
                P.op("pe", (lambda e: e.matmul(sreg[:, 0:512], lhsT=identb, rhs=Bt[hb][:, v0 * 128:v0 * 128 + 512], start=True, stop=False, skip_group_check=True)),
                     reads=rdb, writes=[sk])
                if nw == 5:
                    P.op("pe", (lambda e: e.matmul(sreg[:, 512:640], lhsT=identb, rhs=Bt[hb][:, (v0 + 4) * 128:(v0 + 5) * 128], start=True, stop=False, skip_group_check=True)),
                         reads=rdb, writes=[sk])
                slots = kts + [18, 19]
                for si, kt in enumerate(slots):
                    first_ctx_boundary = (nw == 4 and si == 4)
                    P.op("pe", (lambda e, si=si, kt=kt, fc=first_ctx_boundary: e.matmul(sreg[:, si * 128:(si + 1) * 128], lhsT=kbT2[pb_][rs, kt * 128:(kt + 1) * 128], rhs=qv, start=fc, stop=True, skip_group_check=True)),
                         reads=rdq, writes=[sk])

            na_norm = []

            def emit_pv_na(i):
                h, jl = steps[i]
                hb = h % 2
                jg, jj = jl // 4, jl % 4
                kts, v0 = slots_of(jl)
                slots = kts + [18, 19]
                ns = len(slots)
                pt = PTn[i % 3]
                ptk = ("ptn", i % 3)
                ob = 4 + (h * 4 + jg) % 2
                for si, kt in enumerate(slots):
                    P.op("pe", (lambda e, si=si, kt=kt: e.matmul(PSB[ob][:, jj * 128:(jj + 1) * 128], lhsT=vbext[:, kt, h // 2, h % 2, :], rhs=pt[:, si * 128:(si + 1) * 128], start=(si == 0), stop=(si == ns - 1), skip_group_check=True)),
                         reads=[ptk, ("vbv", kt), ("vb1", kt)], writes=[psk(ob)])
                if jj == 3:
                    orow = slice(0, 64) if hb == 0 else slice(64, 128)
                    lrow = slice(64, 128) if hb == 0 else slice(0, 64)
                    g = (h * 4 + jg) % 2
                    c = h // 2
                    for q4 in range(4):
                        cs = slice(q4 * 128, (q4 + 1) * 128)
                        na_norm.append(lambda cs=cs, q4=q4: P.op("dve", (lambda e: e.reciprocal(out=recip[g][lrow, cs], in_=PSB[ob][lrow, cs])),
                                                               reads=[psk(ob)], writes=[("recip", g, q4)]))
                    na_norm.append(lambda: P.op("dve", (lambda e: e.tensor_tensor(out=tn[g][orow, :], in0=PSB[ob][orow, :], in1=recip[g][lrow, :], op=ALU.mult)),
                                                reads=[psk(ob)] + [("recip", g, q4) for q4 in range(4)], writes=[("tn", g)]))
                    na_norm.append(lambda: P.op("pool", (lambda e: e.tensor_tensor(out=mbT[orow, c, jg * 512:(jg + 1) * 512], in0=tn[g][orow, :], in1=mbT[orow, c, jg * 512:(jg + 1) * 512], op=ALU.mult)),
                                                reads=[("tn", g), ("mT", 1, c, jg)], writes=[("mT", 1, c, jg)]))

            for t in (0, 1, 2, 3, 4, 18, 19):
                build_v(t)
            s_mm(0)
            for i, (h, jl) in enumerate(steps):
                hb = h % 2
                jg, jj = jl // 4, jl % 4
                if i + 1 < len(steps):
                    s_mm(i + 1)
                if i >= 1:
                    emit_pv_na(i - 1)
                if h == 0 and jl + 5 <= 17:
                    build_v(jl + 5)
                kts, v0 = slots_of(jl)
                slots = kts + [18, 19]
                ns = len(slots)
                pt = PTn[i % 3]
                ptk = ("ptn", i % 3)
                sreg = PS2[i % 2]
                P.op("act", (lambda e, pt=pt, sreg=sreg, ns=ns: e.activation(out=pt[:, 0:ns * 128], in_=sreg[:, 0:ns * 128], func=AF.Exp)),
                     reads=[("spair", i % 2)], writes=[ptk])
                for _ in range(2):
                    if na_norm:
                        na_norm.pop(0)()
                if jl == 8 and h == 6:
                    load_w(0)
                if jl == 8 and h == 7:
                    load_w(1)
                if jl == 4 and h + 1 < 8:
                    P.dma("pool", Bt[(h + 1) % 2], nab_d[h + 1], writes=[("bt", (h + 1) % 2)])
                    if h % 2 == 0 and h + 2 < 8:
                        build_pair(h // 2 + 1)
            emit_pv_na(len(steps) - 1)
            while na_norm:
                na_norm.pop(0)()

        if debug_phase is None or debug_phase >= 4:
            phase4()
        if debug_phase == 4:
            for c in range(4):
                dbg_items.append((mbT[:, c, 0:1024], 1024, [("mT", 1, c, 0), ("mT", 1, c, 1)]))
                dbg_items.append((mbT[:, c, 1024:2048], 1024, [("mT", 1, c, 2), ("mT", 1, c, 3)]))

        def phase5():
            P.barrier()
            TX.reset()
            TX.limit = ARENA_BYTES - 12288
            mT = TX.bf(8 * NOWN).rearrange("p (c t) -> p c t", c=8)
            wout = TX.bf(8 * D).rearrange("p (k n) -> p k n", k=8)
            woutf = TX.f32(2 * D).rearrange("p (k n) -> p k n", k=2)
            sga = [TX.bf(512) for _ in range(2)]
            sgb = [TX.bf(512) for _ in range(2)]
            t1 = [TX.f32(512) for _ in range(2)]
            t2 = [TX.f32(512) for _ in range(2)]
            xt = [TX.f32(D) for _ in range(3)]
            rr = [TX.f32(D) for _ in range(3)]
            junk = TX.bf(D)
            ot = [TX.f32(D) for _ in range(2)]
            dg = [TX.f32(128) for _ in range(2)]

            wout_v = wout_d.rearrange("(k p) n -> p k n", p=128)
            gstg = TX.f32(8 * 256).rearrange("p (k n) -> p k n", k=8)
            P.dma("sp", rr[1][0:1, :], bgate_d[:, :], writes=[(("rr", 1), 0), (("rr", 1), 1)])

            gstg_flat = gstg.rearrange("p k n -> p (k n)")
            gbuf = [gstg_flat[:, 0:1024], gstg_flat[:, 1024:2048]]
            gacc = rr[0]

            def gate_load(k):
                P.dma("sp", gbuf[k % 2], wmod_d[k * 128:(k + 1) * 128, 2048:3072], writes=[("gbuf", k % 2)])

            def gate_acc(k):
                if k == 0:
                    P.op("dve", (lambda e: e.tensor_scalar(out=gacc, in0=gbuf[0], scalar1=sT[:, 0, 0:1], scalar2=None, op0=ALU.mult)),
                         reads=[("gbuf", 0), "sT"], writes=[(("rr", 0), 0), (("rr", 0), 1)])
                else:
                    P.op("dve", (lambda e: e.scalar_tensor_tensor(out=gacc, in0=gbuf[k % 2], scalar=sT[:, k, 0:1], in1=gacc, op0=ALU.mult, op1=ALU.add)),
                         reads=[("gbuf", k % 2), "sT", (("rr", 0), 0), (("rr", 0), 1)], writes=[(("rr", 0), 0), (("rr", 0), 1)])

            def gate_bias():
                P.op("dve", (lambda e: e.tensor_tensor(out=gacc[0:1, :], in0=gacc[0:1, :], in1=rr[1][0:1, :], op=ALU.add)),
                     reads=[(("rr", 0), 0), (("rr", 0), 1), (("rr", 1), 0), (("rr", 1), 1)], writes=[(("rr", 0), 0), (("rr", 0), 1)])

            def gate_finish():
                for n in range(2):
                    P.op("pe", (lambda e, n=n: e.matmul(PSB[6 + n], lhsT=ones128, rhs=gacc[:, n * 512:(n + 1) * 512], start=True, stop=True)),
                         reads=["ones128", (("rr", 0), 0), (("rr", 0), 1)], writes=[psk(6 + n)])
                    P.op("dve", (lambda e, n=n: e.tensor_copy(out=gate_b[:, n * 512:(n + 1) * 512], in_=PSB[6 + n])),
                         reads=[psk(6 + n)], writes=[("gate_b", n)])

            def fold_step(k):
                if k % 2 == 0:
                    P.dma("sp", woutf, wout_v[:, k:k + 2, :], writes=["woutf"])
                P.op("dve", (lambda e: e.tensor_tensor(out=wout[:, k, :], in0=woutf[:, k % 2, :], in1=gate_b, op=ALU.mult)),
                     reads=["woutf", ("gate_b", 0), ("gate_b", 1)], writes=[("wout", k)])

            fold_q = []

            for c in range(8):
                wb = wgs[c % 2]
                wk = ("wg", c % 2)
                if c < 4:
                    gate_load(2 * c)
                    gate_load(2 * c + 1)
                for blk in range(4):
                    i = c * 4 + blk
                    pya = (i % 2) * 2
                    pyb = (i % 2) * 2 + 1
                    cs = slice(blk * 512, (blk + 1) * 512)
                    hk = [("hk", s) for s in range(blk * 4, blk * 4 + 4)]
                    for k in range(4):
                        P.op("pe", (lambda e, k=k, wb=wb, cs=cs, pya=pya: e.matmul(PSB[pya], lhsT=wb["oa"][:, k, :], rhs=maT[:, k, cs], start=(k == 0), stop=(k == 3))),
                             reads=[(wk, "oa"), ("mT", 0, k, blk)], writes=[psk(pya)])
                    for k in range(4):
                        P.op("pe", (lambda e, k=k, wb=wb, cs=cs, pyb=pyb: e.matmul(PSB[pyb], lhsT=wb["ob"][:, k, :], rhs=mbT[:, k, cs], start=(k == 0), stop=(k == 3))),
                             reads=[(wk, "ob"), ("mT", 1, k, blk)], writes=[psk(pyb)])
                    for k in range(8):
                        P.op("pe", (lambda e, k=k, wb=wb, cs=cs: e.matmul(PSB[4], lhsT=wb["ga"][:, k, :], rhs=hT_keep[:, k, cs], start=(k == 0), stop=(k == 7))),
                             reads=[(wk, "ga")] + hk, writes=[psk(4)])
                    P.op("act", (lambda e, i=i: e.activation(out=sga[i % 2], in_=PSB[4], func=AF.Sigmoid)), reads=[psk(4)], writes=[("sga", i % 2)])
                    for k in range(8):
                        P.op("pe", (lambda e, k=k, wb=wb, cs=cs: e.matmul(PSB[5], lhsT=wb["gb"][:, k, :], rhs=hT_keep[:, k, cs], start=(k == 0), stop=(k == 7))),
                             reads=[(wk, "gb")] + hk, writes=[psk(5)])
                    P.op("act", (lambda e, i=i: e.activation(out=sgb[i % 2], in_=PSB[5], func=AF.Sigmoid)), reads=[psk(5)], writes=[("sgb", i % 2)])
                    P.op("dve", (lambda e, i=i, pya=pya: e.tensor_tensor(out=t1[i % 2], in0=PSB[pya], in1=sga[i % 2], op=ALU.mult)),
                         reads=[psk(pya), ("sga", i % 2)], writes=[("t1", i % 2)])
                    P.op("dve", (lambda e, i=i, pyb=pyb: e.tensor_tensor(out=t2[i % 2], in0=PSB[pyb], in1=sgb[i % 2], op=ALU.mult)),
                         reads=[psk(pyb), ("sgb", i % 2)], writes=[("t2", i % 2)])
                    P.op("pool", (lambda e, i=i, c=c, cs=cs: e.tensor_tensor(out=mT[:, c, cs], in0=t1[i % 2], in1=t2[i % 2], op=ALU.add)),
                         reads=[("t1", i % 2), ("t2", i % 2)], writes=[("mm", c, blk)])
                    if fold_q:
                        fold_step(fold_q.pop(0))
                    if c < 4 and blk in (1, 3):
                        gate_acc(2 * c + blk // 2)
                    if c == 4 and blk == 0:
                        gate_bias()
                        gate_finish()
                        fold_q.extend(range(8))
                if c + 2 < 8:
                    load_w(c + 2)

            def y_front(jl):
                xb = xt[jl % 3]
                xk = ("xt5", jl % 3)
                rb = rr[jl % 3]
                rk = ("rr", jl % 3)
                for n in range(2):
                    for c in range(8):
                        P.op("pe", (lambda e, c=c, n=n: e.matmul(PSB[6 + n], lhsT=mT[:, c, jl * 128:(jl + 1) * 128], rhs=wout[:, c, n * 512:(n + 1) * 512], start=(c == 0), stop=(c == 7))),
                             reads=[("wout", c), ("mm", c, jl // 4)], writes=[psk(6 + n)])
                    P.op("dve", (lambda e, n=n: e.tensor_tensor(out=rb[:, n * 512:(n + 1) * 512], in0=PSB[6 + n], in1=xb[:, n * 512:(n + 1) * 512], op=ALU.add)),
                         reads=[psk(6 + n), xk], writes=[(rk, n)])
                sc_ = NTILE + jl
                P.op("act", (lambda e: e.activation(out=junk, in_=rb, func=AF.Square, accum_out=ssq[:, sc_:sc_ + 1])),
                     reads=[(rk, 0), (rk, 1)], writes=["junk5", ("ssq", sc_)])
                P.op("pool", (lambda e: e.tensor_scalar(out=rsv[:, sc_:sc_ + 1], in0=ssq[:, sc_:sc_ + 1], scalar1=1.0 / D, scalar2=EPS, op0=ALU.mult, op1=ALU.add)),
                     reads=[("ssq", sc_)], writes=[("rsv", sc_)])
                P.op("pool", (lambda e: e.tensor_tensor(out=rstd[:, sc_:sc_ + 1], in0=rsv[:, sc_:sc_ + 1], in1=mhalf[:, 0:1], op=ALU.pow)),
                     reads=[("rsv", sc_), "mhalf"], writes=[("rstd", sc_)])

            def y_back(jl):
                rb = rr[jl % 3]
                rk = ("rr", jl % 3)
                sc_ = NTILE + jl
                ob_ = ot[jl % 2]
                ok = ("ot", jl % 2)
                P.op("dve", (lambda e: e.scalar_tensor_tensor(out=ob_, in0=rb, scalar=rstd[:, sc_:sc_ + 1], in1=fg_b, op0=ALU.mult, op1=ALU.mult)),
                     reads=[(rk, 0), (rk, 1), ("rstd", sc_), "fg_b"], writes=[ok])
                P.dma("sp", out_d[jl * 128:(jl + 1) * 128, :], ob_, reads=[ok], is_output=True)

            for jl in range(3):
                P.dma("sp", xt[jl % 3], xs[jl * 128:(jl + 1) * 128, :], writes=[("xt5", jl % 3)])
            for it in range(17):
                if it < 16:
                    y_front(it)
                    if it + 3 < 16:
                        P.dma("sp", xt[(it + 3) % 3], xs[(it + 3) * 128:(it + 4) * 128, :], writes=[("xt5", (it + 3) % 3)])
                if it >= 1:
                    y_back(it - 1)

        if debug_phase is None:
            phase5()

        if debug_phase is not None:
            P.barrier()
            T.reset()
            col = 0
            stgs = [T.f32(1024) for _ in range(2)]
            ns = 0
            for ap2, n, rd in dbg_items:
                for o in range(0, n, 1024):
                    m = min(1024, n - o)
                    stg = stgs[ns % 2][:, 0:m]
                    sk_ = ("dbgs", ns % 2)
                    ns += 1
                    P.op("dve", (lambda e, ap2=ap2, stg=stg, o=o, m=m: e.tensor_copy(out=stg, in_=ap2[:, o:o + m])), reads=rd, writes=[sk_])
                    P.dma("sp", dbg_d[:, col:col + m], stg, reads=[sk_], is_output=True)
                    col += m
            assert col <= DBGN

        run = P.emit(sems, dsems)

        @block.sync
        def _(e):
            run("sp", e, final_wait=True)

        @block.scalar
        def _(e):
            run("act", e)

        @block.vector
        def _(e):
            run("dve", e)

        @block.gpsimd
        def _(e):
            run("pool", e)

        @block.tensor
        def _(e):
            run("pe", e)
    return nc


def _gtile(hf, L):
    return L if hf == 0 else 31 - L


def _stage_tiles(hf):
    tiles = []
    for s in range(NTILE):
        if s < 18:
            tiles.append(("g", _gtile(hf, s)))
        elif s < 20:
            tiles.append(("c", s - 18))
        else:
            tiles.append(("g", _gtile(hf, s - 2)))
    return tiles


def _rope_tables(hf):
    half = 8
    freqs = np.float32(10000.0) ** (-(np.arange(half, dtype=np.float32) / np.float32(half)))
    tC = np.ones((32, NKEY), np.float32)
    tS = np.zeros((32, NKEY), np.float32)
    for s, (kind, g) in enumerate(_stage_tiles(hf)):
        if kind != "g":
            continue
        t = g * 128 + np.arange(128)
        row = (t // GRID_W).astype(np.float32)
        col = (t % GRID_W).astype(np.float32)
        for d in range(32):
            pos = row if d < 16 else col
            ang = (pos * freqs[d % 8]).astype(np.float32)
            sgn = -1.0 if (d % 16) < 8 else 1.0
            tC[d, s * 128:(s + 1) * 128] = np.cos(ang)
            tS[d, s * 128:(s + 1) * 128] = sgn * np.sin(ang)
    return tC, tS


_PARTNER = np.array([d + 8 if (d % 16) < 8 else d - 8 for d in range(32)])


def _na_bias(rpb, hf):
    out = np.full((8, 128, NVAR * 128), NEG, np.float32)
    variants = [(0, kt) for kt in range(4)] + [(1, kt) for kt in range(4)] + [(5, 3 + i) for i in range(5)]
    p = np.arange(128)
    for v, (jl, kt) in enumerate(variants):
        gq = _gtile(hf, jl)
        gk = _gtile(hf, kt)
        r = 2 * gq + p // 64
        c = p % 64
        kr = 2 * gk + p // 64
        kc = p % 64
        r_start = np.clip(r - 4, 0, 56)
        c_start = np.clip(c - 8, 0, 48)
        dr = kr[:, None] - r[None, :]
        dc = kc[:, None] - c[None, :]
        valid = (kr[:, None] >= r_start[None, :]) & (kr[:, None] < r_start[None, :] + 8) & \
                (kc[:, None] >= c_start[None, :]) & (kc[:, None] < c_start[None, :] + 16)
        ri = np.clip(dr + 7, 0, 14)
        ci = np.clip(dc + 15, 0, 30)
        for h in range(8):
            g = rpb[h][ri, ci]
            out[h, :, v * 128:(v + 1) * 128] = np.where(valid, g, np.float32(NEG))
    return out


def prep_core(inputs, core):
    b, hf = core // 2, core % 2
    f = lambda a: np.ascontiguousarray(np.asarray(a, dtype=np.float32))
    x = np.asarray(inputs["x"]); ctx = np.asarray(inputs["ctx"])
    parts = []
    for kind, g in _stage_tiles(hf):
        parts.append(x[b, g * 128:(g + 1) * 128] if kind == "g" else ctx[b, g * 128:(g + 1) * 128])
    xs = np.concatenate(parts, axis=0)
    c = np.asarray(inputs["c"])[b]; c_ctx = np.asarray(inputs["c_ctx"])
    cT = np.stack([c.reshape(8, 128).T, c_ctx.reshape(8, 128).T], axis=2).reshape(128, 16)
    w_in = np.asarray(inputs["w_in"])[0]
    w_uq = np.asarray(inputs["w_uq"])[0]
    wuqx = np.concatenate([np.concatenate([w_uq[:, h * 96:(h + 1) * 96], w_uq[:, h * 96 + 64 + _PARTNER]], axis=1) for h in range(8)], axis=1)
    tC, tS = _rope_tables(hf)
    b_mod = np.asarray(inputs["b_mod"])[0]
    return {
        "xs": f(xs),
        "cT": f(cT),
        "wmod": f(np.asarray(inputs["w_mod"])[0]),
        "bcol": f(b_mod.reshape(24, 128).T),
        "bgate": f(b_mod[2048:3072].reshape(1, D)),
        "gT": f(np.asarray(inputs["norm_g"])[0].reshape(8, 128).T),
        "w_in": f(w_in),
        "wkrsw": f(w_in[:, 128 + _PARTNER]),
        "gckv": f(np.asarray(inputs["g_ckv"])[0].reshape(128, 1)),
        "gcq": f(np.asarray(inputs["g_cq"])[0].reshape(2, 128).T),
        "wuq": f(wuqx),
        "wukv": f(np.asarray(inputs["w_ukv"])[0]),
        "tabC": f(tC),
        "tabS": f(tS),
        "nab": f(_na_bias(np.asarray(inputs["rpb"])[0], hf)),
        "woa": f(np.asarray(inputs["w_oa"])[0]),
        "wob": f(np.asarray(inputs["w_ob"])[0]),
        "wout": f(np.asarray(inputs["w_out"])[0]),
        "fgb": f(np.broadcast_to(np.asarray(inputs["final_g"])[None, :], (128, D))),
        "ident": np.eye(128, dtype=np.float32),
    }


def kernel(**inputs):
    nc = build_nc(None)
    in_maps = [prep_core(inputs, core) for core in range(8)]
    res = run_bass_kernel_spmd(nc, in_maps, core_ids=list(range(8)))
    out = np.zeros((4, SEQ, D), np.float32)
    for core in range(8):
        b, hf = core // 2, core % 2
        o = np.asarray(res.results[core]["out"])
        for jl in range(16):
            g = _gtile(hf, jl)
            out[b, g * 128:(g + 1) * 128] = o[jl * 128:(jl + 1) * 128]
    return out
```

```python
import numpy as np
from contextlib import ExitStack
import concourse.bass as bass
import concourse.mybir as mybir
from concourse.bass_utils import run_bass_kernel_spmd

F32 = mybir.dt.float32
BF16 = mybir.dt.bfloat16
AF = mybir.ActivationFunctionType
ALU = mybir.AluOpType

D = 1024
SEQ = 4096
CTXL = 256
GRID_W = 64
NTILE = 34
NKEEP = 20
NKEY = NTILE * 128
NOWN = 2048
EPS = 1e-6
MLA_SCALE = 96.0 ** -0.5
NA_SCALE = 64.0 ** -0.5
NEG = -30000.0
NVAR = 13

DEBUG_PHASE = None

COMPUTE = ("pe", "act", "dve", "pool")
QUEUES = ("sp", "act", "pool")
ALLENG = ("pe", "act", "dve", "pool", "sp")
NDMASEM = 12


class Op:
    __slots__ = ("eng", "fn", "deps", "sig", "count", "is_dma", "sem", "val", "prev_dma")

    def __init__(self, eng, fn, is_dma=False):
        self.eng = eng
        self.fn = fn
        self.deps = []
        self.sig = False
        self.count = None
        self.is_dma = is_dma
        self.sem = None
        self.val = None
        self.prev_dma = None


class Prog:
    def __init__(self):
        self.ops = {e: [] for e in ALLENG}
        self.order = []
        self.last_w = {}
        self.readers = {}
        self.ndma = {q: 0 for q in QUEUES}
        self.dma_hist = {q: [] for q in QUEUES}
        self.pending_barrier = {}
        self.out_dmas = []

    def _track(self, op, reads, writes):
        deps = []
        for k in reads:
            w = self.last_w.get(k)
            if w is not None:
                deps.append((w, "raw"))
        for k in writes:
            w = self.last_w.get(k)
            if w is not None:
                deps.append((w, "waw"))
            for r in self.readers.get(k, ()):
                deps.append((r, "war"))
        for k in reads:
            self.readers.setdefault(k, []).append(op)
        for k in writes:
            self.last_w[k] = op
            self.readers[k] = []
        seen = set()
        for d, kind in deps:
            if d is op or id(d) in seen:
                continue
            if (not d.is_dma) and (not op.is_dma) and d.eng == op.eng:
                if kind != "raw" or op.eng == "pe":
                    continue
            seen.add(id(d))
            op.deps.append(d)
        b = self.pending_barrier.pop(op.eng, None)
        if b:
            for d in b:
                if d is op or id(d) in seen:
                    continue
                if d.eng == op.eng and not d.is_dma and not op.is_dma:
                    continue
                seen.add(id(d))
                op.deps.append(d)

    def op(self, eng, fn, reads=(), writes=()):
        o = Op(eng, fn)
        self._track(o, reads, writes)
        self.ops[eng].append(o)
        self.order.append(o)
        return o

    def dma(self, queue, out, in_, reads=(), writes=(), is_output=False):
        o = Op(queue, lambda e: e.dma_start(out=out, in_=in_), is_dma=True)
        k = self.ndma[queue]
        self.ndma[queue] = k + 1
        o.sem = (queue, k % NDMASEM)
        o.val = 16 * (k // NDMASEM + 1)
        hist = self.dma_hist[queue]
        if k >= NDMASEM:
            o.prev_dma = hist[k - NDMASEM]
        hist.append(o)
        self._track(o, reads, writes)
        if o.prev_dma is not None and all(d is not o.prev_dma for d in o.deps):
            o.deps.append(o.prev_dma)
        self.ops[queue].append(o)
        self.order.append(o)
        if is_output:
            self.out_dmas.append(o)
        return o

    def barrier(self):
        last = []
        for e in ALLENG:
            for o in reversed(self.ops[e]):
                if not o.is_dma:
                    last.append(o)
                    break
        for q in QUEUES:
            last.extend(self.dma_hist[q][-NDMASEM:])
        for e in ALLENG:
            self.pending_barrier[e] = list(last)

    def emit(self, sems, dsems):
        for o in self.order:
            for d in o.deps:
                d.sig = True
        for e in COMPUTE:
            c = 0
            for o in self.ops[e]:
                if o.is_dma:
                    continue
                if o.sig:
                    c += 1
                    o.count = c
        prog = self

        def run(eng_name, e, final_wait=False):
            waited = {}
            for o in prog.ops[eng_name]:
                for d in o.deps:
                    if d.is_dma:
                        key, val, sem = d.sem, d.val, dsems[d.sem]
                    else:
                        key, val, sem = d.eng, d.count, sems[d.eng]
                    if waited.get(key, 0) >= val:
                        continue
                    waited[key] = val
                    e.wait_ge(sem, val)
                ins = o.fn(e)
                if o.is_dma:
                    ins.then_inc(dsems[o.sem], 16)
                elif o.sig:
                    ins.then_inc(sems[o.eng], 1)
            if final_wait:
                for d in prog.out_dmas:
                    if waited.get(d.sem, 0) >= d.val:
                        continue
                    waited[d.sem] = d.val
                    e.wait_ge(dsems[d.sem], d.val)

        return run


class Bump:
    def __init__(self, arena, base, limit):
        self.arena = arena
        self.base = base
        self.limit = limit
        self.off = base

    def reset(self):
        self.off = self.base

    def _take(self, nbytes):
        off = (self.off + 63) // 64 * 64
        assert off + nbytes <= self.limit, (off, nbytes, self.limit)
        self.off = off + nbytes
        return off

    def bf(self, n):
        off = self._take(2 * n)
        return self.arena[:, off // 2: off // 2 + n]

    def f32(self, n):
        off = self._take(4 * n)
        return self.arena[:, off // 2: off // 2 + 2 * n].bitcast(F32)


ARENA_BYTES = 207 * 1024
PERS_END = 86016
XREG_END = PERS_END + 50688


def build_nc(debug_phase=None):
    nc = bass.Bass("TRN2", target_bir_lowering=False)

    def din(name, shape):
        return nc.dram_tensor(name, list(shape), F32, kind="ExternalInput").ap()

    xs = din("xs", [NKEY, D])
    cT_d = din("cT", [128, 16])
    wmod_d = din("wmod", [D, 3 * D])
    bcol_d = din("bcol", [128, 24])
    bgate_d = din("bgate", [1, D])
    gT_d = din("gT", [128, 8])
    win_d = din("w_in", [D, 5024])
    wkrsw_d = din("wkrsw", [D, 32])
    gckv_d = din("gckv", [128, 1])
    gcq_d = din("gcq", [128, 2])
    wuq_d = din("wuq", [256, 1024])
    wukv_d = din("wukv", [128, 1024])
    tabC_d = din("tabC", [32, NKEY])
    tabS_d = din("tabS", [32, NKEY])
    nab_d = din("nab", [8, 128, NVAR * 128])
    woa_d = din("woa", [512, D])
    wob_d = din("wob", [512, D])
    wout_d = din("wout", [D, D])
    fgb_d = din("fgb", [128, D])
    ident_d = din("ident", [128, 128])
    out_d = nc.dram_tensor("out", [NOWN, D], F32, kind="ExternalOutput").ap()
    DBGN = 16384
    dbg_d = None
    if debug_phase is not None:
        dbg_d = nc.dram_tensor("dbg", [128, DBGN], F32, kind="ExternalOutput").ap()

    win_v = win_d.rearrange("(k p) n -> p k n", p=128)

    with ExitStack() as st:
        E = st.enter_context
        AR = E(nc.sbuf_tensor("arena", [128, ARENA_BYTES // 2], BF16))[:]
        PS2A = E(nc.psum_tensor("ps01", [128, 1024], F32))[:]
        PS2B = E(nc.psum_tensor("ps23", [128, 1024], F32))[:]
        PS2 = [PS2A, PS2B]
        PSB = [PS2A[:, 0:512], PS2A[:, 512:1024], PS2B[:, 0:512], PS2B[:, 512:1024]]
        PSB += [E(nc.psum_tensor(f"psb{i}", [128, 512], F32))[:] for i in range(4, 8)]
        sems = {e: E(nc.semaphore("s_" + e)) for e in COMPUTE}
        dsems = {(q, i): E(nc.semaphore(f"d_{q}_{i}")) for q in QUEUES for i in range(NDMASEM)}
        block = E(nc.Block())
        P = Prog()

        def psk(i):
            return "ps%d" % i

        def ps_bf(i):
            return PSB[i].bitcast(BF16)[:, 0:1024].rearrange("p (k t) -> p k t", k=8)

        pers = Bump(AR, 0, PERS_END)
        hT_keep = pers.bf(8 * NKEEP * 128).rearrange("p (k t) -> p k t", k=8)
        maT = pers.bf(4 * NOWN).rearrange("p (c t) -> p c t", c=4)
        mbT = pers.bf(4 * NOWN).rearrange("p (c t) -> p c t", c=4)
        gate_b = pers.f32(D)
        fg_b = pers.f32(D)
        identb = pers.bf(128)
        idf = pers.f32(128)
        ones128 = pers.f32(128)
        SC = pers.f32(16).rearrange("p (k j) -> p k j", j=2)
        SH = pers.f32(16).rearrange("p (k j) -> p k j", j=2)
        colmod = pers.f32(32).rearrange("p (k j) -> p k j", j=2)
        gT = pers.f32(8)
        gckv = pers.f32(1)
        gcq = pers.f32(2)
        mhalf = pers.f32(1)
        ssq = pers.f32(NTILE + 16)
        rsv = pers.f32(NTILE + 16)
        rstd = pers.f32(NTILE + 16)
        cT = pers.f32(16).rearrange("p (k j) -> p k j", j=2)
        sT = pers.f32(16).rearrange("p (k j) -> p k j", j=2)

        xreg = Bump(AR, PERS_END, XREG_END)
        ckvnT = xreg.bf(NKEY)
        cqnT = xreg.bf(2 * NOWN).rearrange("p (k t) -> p k t", k=2)
        Kt = [xreg.bf(NKEY), xreg.bf(NKEY)]
        tabC_own = xreg.f32(NOWN)
        tabS_own = xreg.f32(NOWN)

        NAW = [cqnT.rearrange("p k t -> p (k t)").rearrange("p (k n) -> p k n", k=8),
               tabC_own.bitcast(BF16).rearrange("p (k n) -> p k n", k=8),
               tabS_own.bitcast(BF16).rearrange("p (k n) -> p k n", k=8)]
        T = Bump(AR, XREG_END, ARENA_BYTES)
        TX = Bump(AR, PERS_END, ARENA_BYTES)

        dbg_items = []

        EPS_AP = pers.f32(1)
        bcol = pers.f32(24)
        colmod3 = pers.f32(48).rearrange("p (j t) -> p j t", t=2)
        TS_BYTES = 16384 + 3072 + 1024 + 2048 + 256
        TS = Bump(AR, ARENA_BYTES - TS_BYTES, ARENA_BYTES)
        T.limit = ARENA_BYTES - TS_BYTES
        wz = [TS.bf(8 * 512).rearrange("p (k n) -> p k n", k=8) for _ in range(2)]
        wuq = TS.bf(2 * 1024).rearrange("p (k n) -> p k n", k=2)
        wukv = TS.bf(1024)
        P.dma("sp", idf, ident_d[:, :], writes=["idf"])
        P.dma("sp", cT.rearrange("p k j -> p (k j)"), cT_d[:, :], writes=["cT"])
        P.dma("sp", gT, gT_d[:, :], writes=["gT"])
        P.dma("sp", bcol, bcol_d[:, :], writes=["bcol"])
        P.dma("sp", gckv, gckv_d[:, :], writes=["gckv"])
        P.dma("sp", gcq, gcq_d[:, :], writes=["gcq"])
        P.dma("sp", fg_b, fgb_d[:, :], writes=["fg_b"])
        P.op("dve", lambda e: e.tensor_copy(out=identb, in_=idf), reads=["idf"], writes=["identb"])
        P.op("pool", lambda e: e.memset(ones128, 1.0), writes=["ones128"])
        P.op("pool", lambda e: e.memset(mhalf, -0.5), writes=["mhalf"])
        P.op("pool", lambda e: e.memset(EPS_AP, EPS), writes=["epsap"])
        P.op("act", lambda e: e.activation(out=sT.rearrange("p k j -> p (k j)"), in_=cT.rearrange("p k j -> p (k j)"), func=AF.Silu),
             reads=["cT"], writes=["sT"])
        WMB = Bump(AR, 40960, 40960 + 32768)
        wm = [WMB.f32(2 * D) for _ in range(3)]
        modrow = WMB.f32(2 * D)
        WM_KEYS = [("wm", i) for i in range(3)] + [("modrow", n) for n in range(16)]
        wmod_v = wmod_d.rearrange("(k p) n -> p k n", p=128)

        XT0 = Bump(AR, XREG_END, ARENA_BYTES)
        xt_pre = [XT0.f32(D) for _ in range(4)]
        for k in range(8):
            buf = wm[k % 3]
            bk = ("wm", k % 3)
            P.dma("sp", buf, wmod_d[k * 128:(k + 1) * 128, 0:2 * D], writes=[bk])
            if k == 2:
                P.dma("sp", xt_pre[0], xs[0:128, :], writes=[("xt", 0)])
                P.dma("sp", xt_pre[1], xs[128:256, :], writes=[("xt", 1)])
            for n in range(4):
                P.op("pe", (lambda e, k=k, n=n, buf=buf: e.matmul(PSB[2 + n][0:2, :], lhsT=sT[:, k, :], rhs=buf[:, n * 512:(n + 1) * 512], start=(k == 0), stop=(k == 7), skip_group_check=True)),
                     reads=["sT", bk], writes=[psk(2 + n)])
        for n in range(4):
            P.op("dve", (lambda e, n=n: e.tensor_copy(out=modrow[0:2, n * 512:(n + 1) * 512], in_=PSB[2 + n][0:2, :])),
                 reads=[psk(2 + n)], writes=[("modrow", 4 * n + i_) for i_ in range(4)])
        pcm = PSB[4][:, 0:32].rearrange("p (j t) -> p j t", t=2)
        for j in range(16):
            P.op("pe", (lambda e, j=j: e.matmul(pcm[:, j, :], lhsT=modrow[0:2, j * 128:(j + 1) * 128], rhs=idf[0:2, 0:2], start=True, stop=True, skip_group_check=True)),
                 reads=[("modrow", j), "idf"], writes=[psk(4)])
        P.op("dve", lambda e: e.tensor_tensor(out=colmod3[:, 0:16, :], in0=pcm, in1=bcol[:, 0:16].unsqueeze(2).broadcast_to([128, 16, 2]), op=ALU.add),
             reads=[psk(4), "bcol"], writes=["colmod"])
        P.op("dve", lambda e: e.tensor_copy(out=SH, in_=colmod3[:, 0:8, :]), reads=["colmod"], writes=["SH"])
        P.op("dve", lambda e: e.scalar_tensor_tensor(out=SC, in0=colmod3[:, 8:16, :], scalar=1.0, in1=gT.unsqueeze(2).broadcast_to([128, 8, 2]), op0=ALU.add, op1=ALU.mult),
             reads=["colmod", "gT"], writes=["SC"])

        if debug_phase == 0:
            dbg_items.append((SC.rearrange("p k j -> p (k j)"), 16, ["SC"]))
            dbg_items.append((SH.rearrange("p k j -> p (k j)"), 16, ["SH"]))

        ZBLK = 4

        def phase1():
            T.reset()
            xt = [T.f32(D) for _ in range(4)]
            junk = T.bf(D)
            xn = [T.bf(D) for _ in range(2)]
            hT_rot = T.bf(8 * 512).rearrange("p (k t) -> p k t", k=8)
            tabC_r = [T.f32(512) for _ in range(1)]
            tabS_r = [T.f32(512) for _ in range(1)]
            wkv = T.bf(8 * 192).rearrange("p (k n) -> p k n", k=8)
            wcq = T.bf(8 * 256).rearrange("p (k n) -> p k n", k=8)
            sq = [T.bf(512) for _ in range(3)]
            onesb = T.bf(128)
            rstd_b = T.f32(512)
            lnv = rstd_b
            t1 = T.f32(512)
            t2 = T.f32(512)
            P.dma("pool", wkv[:, :, 0:160], win_v[:, :, 0:160], writes=["wkv_a"])
            P.dma("pool", wkv[:, :, 160:192], wkrsw_d.rearrange("(k p) n -> p k n", p=128), writes=["wkv_b"])
            P.dma("pool", wcq, win_v[:, :, 1184:1440], writes=["wcq"])
            P.op("pool", lambda e: e.memset(onesb, 1.0), writes=["onesb"])
            nrest = [0]

            def stage_c(s0, ntl):
                N = ntl * 128
                c0 = s0 * 128
                keep = s0 < NKEEP
                if keep:
                    hsrc = lambda k: hT_keep[:, k, c0:c0 + N]
                    hkeys = [("hk", s) for s in range(s0, s0 + ntl)]
                else:
                    hsrc = lambda k: hT_rot[:, k, 0:N]
                    hkeys = ["hrot"]
                own = c0 < NOWN
                if own:
                    tC = tabC_own[64:96, c0:c0 + N]
                    tS = tabS_own[64:96, c0:c0 + N]
                    tkeys = ["tabC_own", "tabS_own"]
                else:
                    tC = tabC_r[0][64:96, 0:N]
                    tS = tabS_r[0][64:96, 0:N]
                    tkeys = [("tabr", 0)]

                def mm_ckv():
                    for k in range(8):
                        P.op("pe", (lambda e, k=k: e.matmul(PSB[2][:, 0:N], lhsT=wkv[:, k, 0:128], rhs=hsrc(k), start=(k == 0), stop=(k == 7))),
                             reads=["wkv_a"] + hkeys, writes=[psk(2)])

                def mm_kr():
                    for k in range(8):
                        P.op("pe", (lambda e, k=k: e.matmul(PSB[3][64:128, 0:N], lhsT=wkv[:, k, 128:192], rhs=hsrc(k), start=(k == 0), stop=(k == 7))),
                             reads=["wkv_a", "wkv_b"] + hkeys, writes=[psk(3)])

                def mm_ksw():
                    pass

                def p1():
                    if not own:
                        P.dma("sp", tC, tabC_d[:, c0:c0 + N], writes=tkeys)
                        P.dma("sp", tS, tabS_d[:, c0:c0 + N], writes=tkeys)
                    mm_ckv()
                    if not keep:
                        mm_kr()
                        mm_ksw()

                def p2():
                    if keep:
                        mm_kr()
                    P.op("act", lambda e: e.activation(out=sq[0][:, 0:N], in_=PSB[2][:, 0:N], func=AF.Square), reads=[psk(2)], writes=["sq0"])

                def p3():
                    if keep:
                        mm_ksw()
                    P.op("pe", lambda e: e.matmul(PSB[5][:, 0:N], lhsT=onesb, rhs=sq[0][:, 0:N], start=True, stop=True),
                         reads=["onesb", "sq0"], writes=[psk(5)])

                def p4():
                    P.op("act", lambda e: e.activation(out=lnv[:, 0:N], in_=PSB[5][:, 0:N], func=AF.Ln, scale=1.0 / 128, bias=EPS_AP), reads=[psk(5), "epsap"], writes=["rstd_b"])
                    P.op("act", lambda e: e.activation(out=rstd_b[:, 0:N], in_=lnv[:, 0:N], func=AF.Exp, scale=-0.5), reads=["rstd_b"], writes=["rstd_b"])
                    P.op("dve", lambda e: e.tensor_tensor(out=t1[64:96, 0:N], in0=PSB[3][64:96, 0:N], in1=tC, op=ALU.mult), reads=[psk(3)] + tkeys, writes=["t1"])
                    P.op("dve", lambda e: e.tensor_tensor(out=t2[64:96, 0:N], in0=PSB[3][96:128, 0:N], in1=tS, op=ALU.mult), reads=[psk(3)] + tkeys, writes=["t2"])
                    P.op("dve", lambda e: e.tensor_tensor(out=Kt[0][64:96, c0:c0 + N], in0=t1[64:96, 0:N], in1=t2[64:96, 0:N], op=ALU.add), reads=["t1", "t2"], writes=[("ktr", 0, s0)])
                    if own:
                        for k2 in range(2):
                            for k in range(8):
                                P.op("pe", (lambda e, k=k, k2=k2: e.matmul(PSB[6 + k2][:, 0:N], lhsT=wcq[:, k, k2 * 128:(k2 + 1) * 128], rhs=hsrc(k), start=(k == 0), stop=(k == 7))),
                                     reads=["wcq"] + hkeys, writes=[psk(6 + k2)])

                def p5():
                    P.op("dve", lambda e: e.scalar_tensor_tensor(out=ckvnT[:, c0:c0 + N], in0=PSB[2][:, 0:N], scalar=gckv[:, 0:1], in1=rstd_b[:, 0:N], op0=ALU.mult, op1=ALU.mult),
                         reads=[psk(2), "gckv", "rstd_b"], writes=[("ckvn", s0)])
                    P.op("pool", lambda e: e.tensor_copy(out=Kt[1][64:96, c0:c0 + N], in_=Kt[0][64:96, c0:c0 + N]), reads=[("ktr", 0, s0)], writes=[("ktr", 1, s0)])
                    if own:
                        for k2 in range(2):
                            P.op("act", (lambda e, k2=k2: e.activation(out=sq[1 + k2][:, 0:N], in_=PSB[6 + k2][:, 0:N], func=AF.Square)), reads=[psk(6 + k2)], writes=["sq%d" % (1 + k2)])

                def p6():
                    if own:
                        for k2 in range(2):
                            P.op("pe", (lambda e, k2=k2: e.matmul(PSB[5][:, 0:N], lhsT=onesb, rhs=sq[1 + k2][:, 0:N], start=(k2 == 0), stop=(k2 == 1))),
                                 reads=["onesb", "sq%d" % (1 + k2)], writes=[psk(5)])

                def p7():
                    if own:
                        P.op("act", lambda e: e.activation(out=lnv[:, 0:N], in_=PSB[5][:, 0:N], func=AF.Ln, scale=1.0 / 256, bias=EPS_AP), reads=[psk(5), "epsap"], writes=["rstd_b"])
                        P.op("act", lambda e: e.activation(out=rstd_b[:, 0:N], in_=lnv[:, 0:N], func=AF.Exp, scale=-0.5), reads=["rstd_b"], writes=["rstd_b"])

                def p8():
                    if own:
                        for k2 in range(2):
                            P.op("dve", (lambda e, k2=k2: e.scalar_tensor_tensor(out=cqnT[:, k2, c0:c0 + N], in0=PSB[6 + k2][:, 0:N], scalar=gcq[:, k2:k2 + 1], in1=rstd_b[:, 0:N], op0=ALU.mult, op1=ALU.mult)),
                                 reads=[psk(6 + k2), "gcq", "rstd_b"], writes=[("cqn", s0)])

                return [p1, p2, p3, p4, p5, p6, p7, p8]

            def load_x(s):
                P.dma("sp", xt[s % 4], xs[s * 128:(s + 1) * 128, :], writes=[("xt", s % 4)])

            def t_square(s):
                xb = xt[s % 4]
                xk = ("xt", s % 4)
                P.op("act", (lambda e: e.activation(out=junk, in_=xb, func=AF.Square, accum_out=ssq[:, s:s + 1])),
                     reads=[xk], writes=["junk", ("ssq", s)])
                P.op("pool", (lambda e: e.tensor_scalar(out=rsv[:, s:s + 1], in0=ssq[:, s:s + 1], scalar1=1.0 / D, scalar2=EPS, op0=ALU.mult, op1=ALU.add)),
                     reads=[("ssq", s)], writes=[("rsv", s)])
                P.op("pool", (lambda e: e.tensor_tensor(out=rstd[:, s:s + 1], in0=rsv[:, s:s + 1], in1=mhalf[:, 0:1], op=ALU.pow)),
                     reads=[("rsv", s), "mhalf"], writes=[("rstd", s)])

            def t_norm(s):
                xb = xt[s % 4]
                P.op("dve", (lambda e: e.tensor_scalar(out=xn[s % 2], in0=xb, scalar1=rstd[:, s:s + 1], scalar2=None, op0=ALU.mult)),
                     reads=[("xt", s % 4), ("rstd", s)], writes=[("xn", s % 2)])

            def t_transpose(s):
                xnb = xn[s % 2]
                pt = ps_bf(s % 2)
                for k in range(8):
                    P.op("pe", (lambda e, k=k: e.transpose(pt[:, k, :], xnb[:, k * 128:(k + 1) * 128], identb)),
                         reads=[("xn", s % 2), "identb"], writes=[psk(s % 2)])

            deferred = []

            def t_evac(s, it):
                pb = s % 2
                pt = ps_bf(pb)
                j = 1 if s in (18, 19) else 0
                for k in range(8):
                    if s < NKEEP:
                        dst = hT_keep[:, k, s * 128:(s + 1) * 128]
                        wkey = ("hk", s)
                    else:
                        r = (s - NKEEP) % 4
                        dst = hT_rot[:, k, r * 128:(r + 1) * 128]
                        wkey = "hrot"
                    if k < 5:
                        P.op("dve", (lambda e, k=k, dst=dst: e.tensor_scalar(out=dst, in0=pt[:, k, :], scalar1=SC[:, k, j:j + 1], scalar2=SH[:, k, j:j + 1], op0=ALU.mult, op1=ALU.add)),
                             reads=[psk(pb), "SC", "SH"], writes=[wkey])
                    else:
                        P.op("act", (lambda e, k=k, dst=dst: e.activation(out=dst, in_=pt[:, k, :], func=AF.Identity, scale=SC[:, k, j:j + 1], bias=SH[:, k, j:j + 1])),
                             reads=[psk(pb), "SC", "SH"], writes=[wkey])
                parts = None
                if s % 4 == 3:
                    parts = stage_c(s - 3, 4)
                elif s == NTILE - 1:
                    parts = stage_c(s - 1, 2)
                if parts:
                    lag = 2 if s == NTILE - 1 else 0
                    for d_, p_ in enumerate(parts):
                        deferred.append((it + 1 + lag + d_, p_))

            P.dma("sp", tabC_own[64:96, :], tabC_d[:, 0:NOWN], writes=["tabC_own"])
            P.dma("sp", tabS_own[64:96, :], tabS_d[:, 0:NOWN], writes=["tabS_own"])
            P.dma("pool", wz[0], win_v[:, :, 1440:1952], writes=["wz0"])
            P.dma("pool", wz[1], win_v[:, :, 2464:2976], writes=["wz1"])
            prefetch = [lambda: P.dma("pool", wuq, wuq_d.rearrange("(k p) n -> p k n", p=128), writes=["wuq"]),
                        lambda: P.dma("pool", wukv, wukv_d[:, :], writes=["wukv"])]
            zq = [(which, c, blk) for blk in range(ZBLK) for which in range(2) for c in range(4)]
            zcount = [0]

            def z_mm(which, c, blk):
                for k in range(8):
                    P.op("pe", (lambda e, k=k: e.matmul(PSB[4], lhsT=wz[which][:, k, c * 128:(c + 1) * 128], rhs=hT_keep[:, k, blk * 512:(blk + 1) * 512], start=(k == 0), stop=(k == 7))),
                         reads=["wz%d" % which] + [("hk", s) for s in range(blk * 4, blk * 4 + 4)], writes=[psk(4)])

            def z_evac(which, c, blk):
                dstT = maT if which == 0 else mbT
                n_ = zcount[0]
                zcount[0] += 1
                if n_ % 2 == 0:
                    P.op("act", (lambda e: e.activation(out=dstT[:, c, blk * 512:(blk + 1) * 512], in_=PSB[4], func=AF.Copy)),
                         reads=[psk(4)], writes=[("mT", which, c, blk)] + WM_KEYS)
                else:
                    P.op("dve", (lambda e: e.tensor_copy(out=dstT[:, c, blk * 512:(blk + 1) * 512], in_=PSB[4])),
                         reads=[psk(4)], writes=[("mT", which, c, blk)] + WM_KEYS)

            for it in range(NTILE + 3):
                deferred.sort(key=lambda t_: t_[0])
                while deferred and deferred[0][0] <= it:
                    deferred.pop(0)[1]()
                if zq and it >= 4 * zq[0][2] + 8:
                    g_ = zq.pop(0)
                    z_mm(*g_)
                    deferred.append((it + 1, (lambda g_=g_: z_evac(*g_))))
                if it < NTILE:
                    t_square(it)
                if 0 <= it - 1 < NTILE:
                    t_norm(it - 1)
                if it + 2 < NTILE:
                    load_x(it + 2)
                if 0 <= it - 2 < NTILE:
                    t_transpose(it - 2)
                if 0 <= it - 3 < NTILE:
                    t_evac(it - 3, it)
                if it >= 8 and it % 4 == 0 and prefetch:
                    prefetch.pop(0)()
            it = NTILE + 3
            while zq or deferred:
                deferred.sort(key=lambda t_: t_[0])
                while deferred and deferred[0][0] <= it:
                    deferred.pop(0)[1]()
                if zq:
                    g_ = zq.pop(0)
                    z_mm(*g_)
                    deferred.append((it + 1, (lambda g_=g_: z_evac(*g_))))
                it += 1
            while prefetch:
                prefetch.pop(0)()

        if debug_phase is None or debug_phase >= 1:
            phase1()
        if debug_phase == 1:
            allk = [("ckvn", s) for s in range(0, NTILE, 4)]
            dbg_items.append((ckvnT, NKEY, allk))
            dbg_items.append((Kt[0], NKEY, [("ktr", 0, s) for s in range(0, NTILE, 4)]))
            dbg_items.append((Kt[1], NKEY, [("ktr", 1, s) for s in range(0, NTILE, 4)]))
            dbg_items.append((cqnT[:, 0, :], NOWN, [("cqn", s) for s in range(0, 16, 4)]))

        def phase2_groups():
            groups = []
            i = 0
            for which in range(2):
                dstT = maT if which == 0 else mbT
                for c in range(4):
                    for blk in range(ZBLK, 4):
                        pb = 4 + i % 2
                        i += 1

                        def grp(which=which, dstT=dstT, c=c, blk=blk, pb=pb):
                            for k in range(8):
                                P.op("pe", (lambda e, k=k: e.matmul(PSB[pb], lhsT=wz[which][:, k, c * 128:(c + 1) * 128], rhs=hT_keep[:, k, blk * 512:(blk + 1) * 512], start=(k == 0), stop=(k == 7))),
                                     reads=["wz%d" % which] + [("hk", s) for s in range(blk * 4, blk * 4 + 4)], writes=[psk(pb)])
                            P.op("act", (lambda e: e.activation(out=dstT[:, c, blk * 512:(blk + 1) * 512], in_=PSB[pb], func=AF.Silu)),
                                 reads=[psk(pb)], writes=[("mT", which, c, blk)] + WM_KEYS)
                        groups.append(grp)
            return groups

        NPRE = 0
        p2groups = phase2_groups() if (debug_phase is None or debug_phase >= 2) else []

        XA = Bump(AR, PERS_END, PERS_END + 8704)
        XB = Bump(AR, PERS_END + 8704 + 8192, PERS_END + 8704 + 8192 + 17408)
        qbT2 = [XA.bf(2048), XA.bf(2048)]
        kbT2 = [XB.bf(NKEEP * 128) for _ in range(2)]
        Bt0 = XB.bf(NVAR * 128)
        NAWQ, NAWK = NAW[1], NAW[2]

        def pair_chunks(hp, bank_fn, extra_q=(), extra_k=()):
            pb_ = hp % 2
            chunks = []
            st_ = {}
            for which, nblk in (("q", 4), ("k", 5)):
                for blk in range(nblk):
                    for k0 in range(0, 8, 2):
                        def ch(which=which, blk=blk, k0=k0):
                            if k0 == 0:
                                st_["pb"] = bank_fn()
                            pb = st_["pb"]
                            w_ = NAWQ if which == "q" else NAWK
                            wkey = "wqb" if which == "q" else "wkb"
                            for k in (k0, k0 + 1):
                                P.op("pe", (lambda e, k=k: e.matmul(PSB[pb], lhsT=w_[:, k, hp * 128:(hp + 1) * 128], rhs=hT_keep[:, k, blk * 512:(blk + 1) * 512], start=(k == 0), stop=(k == 7))),
                                     reads=[wkey] + [("hk", s) for s in range(blk * 4, blk * 4 + 4)], writes=[psk(pb)])
                            if k0 == 6:
                                if which == "q":
                                    P.op("dve", (lambda e: e.tensor_scalar(out=qbT2[pb_][:, blk * 512:(blk + 1) * 512], in0=PSB[pb], scalar1=NA_SCALE, scalar2=None, op0=ALU.mult)),
                                         reads=[psk(pb)], writes=[("qbT", pb_)] + list(extra_q))
                                else:
                                    P.op("dve", (lambda e: e.tensor_copy(out=kbT2[pb_][:, blk * 512:(blk + 1) * 512], in_=PSB[pb])),
                                         reads=[psk(pb)], writes=[("kbT", pb_)] + list(extra_k))
                        chunks.append(ch)
            return chunks

        def phase3():
            for _ in range(NPRE):
                p2groups.pop(0)()
            T.reset()
            Vext = [T.bf(NTILE * 128).rearrange("p (t c) -> p t c", c=128) for _ in range(2)]
            wz0_flat = wz[0].rearrange("p k n -> p (k n)")
            Qt = [wz0_flat[:, 0:NOWN], wz0_flat[:, NOWN:2 * NOWN]]
            PT = [T.bf(1024) for _ in range(3)]
            wz1_flat = wz[1].rearrange("p k n -> p (k n)")
            t1 = wz1_flat[:, 0:1024].bitcast(F32)
            t2 = wz1_flat[:, 1024:2048].bitcast(F32)
            recip = [T.f32(512) for _ in range(2)]
            tn = [T.f32(512) for _ in range(2)]
            BB = [6, 7]
            bbi = [0]

            def nextbb():
                b = BB[bbi[0] % len(BB)]
                bbi[0] += 1
                return b

            def build_chunks(h):
                hb = h % 2
                voff = 0 if hb == 0 else 64
                chunks = []

                def k_chunk(blk):
                    c0 = blk * 512
                    N = min(512, NKEY - c0)
                    pb = nextbb()
                    P.op("pe", (lambda e: e.matmul(PSB[pb][0:64, 0:N], lhsT=wukv[:, h * 128:h * 128 + 64], rhs=ckvnT[:, c0:c0 + N], start=True, stop=True)),
                         reads=["wukv", ("ckvn", blk * 4)], writes=[psk(pb)])
                    P.op("dve", (lambda e: e.tensor_copy(out=Kt[hb][0:64, c0:c0 + N], in_=PSB[pb][0:64, 0:N])),
                         reads=[psk(pb)], writes=[("ktn", hb)])

                for blk in range(9):
                    chunks.append(lambda blk=blk: k_chunk(blk))

                state = {}

                def v_chunk(g0, i, ng):
                    if i == 0:
                        state["pb"] = nextbb()
                    pb = state["pb"]
                    pv = PSB[pb].rearrange("p (t c) -> p t c", c=64)
                    tl = g0 + i
                    P.op("pe", (lambda e: e.matmul(pv[:, i, :], lhsT=ckvnT[:, tl * 128:(tl + 1) * 128], rhs=wukv[:, h * 128 + 64:h * 128 + 128], start=True, stop=True, skip_group_check=True)),
                         reads=["wukv", ("ckvn", (tl // 4) * 4 if tl < 32 else 32)], writes=[psk(pb)])
                    if i == ng - 1:
                        P.op("dve", (lambda e: e.tensor_copy(out=Vext[hb][:, g0:g0 + ng, voff:voff + 64], in_=pv[:, 0:ng, :])),
                             reads=[psk(pb)], writes=[("vextv", hb)])

                for g0 in range(0, NTILE, 8):
                    ng = min(8, NTILE - g0)
                    for i in range(0, ng, 2):
                        def two(g0=g0, i=i, ng=ng):
                            v_chunk(g0, i, ng)
                            if i + 1 < ng:
                                v_chunk(g0, i + 1, ng)
                        chunks.append(two)

                def q_chunk(blk):
                    c0 = blk * 512
                    pa = nextbb()
                    for k in range(2):
                        P.op("pe", (lambda e, k=k: e.matmul(PSB[pa], lhsT=wuq[:, k, h * 128:(h + 1) * 128], rhs=cqnT[:, k, c0:c0 + 512], start=(k == 0), stop=(k == 1))),
                             reads=["wuq", ("cqn", blk * 4)], writes=[psk(pa)])
                    P.op("dve", (lambda e: e.tensor_copy(out=Qt[hb][0:64, c0:c0 + 512], in_=PSB[pa][0:64, :])),
                         reads=[psk(pa)], writes=[("qt", hb), "wz0"])
                    P.op("dve", (lambda e: e.tensor_tensor(out=t1[64:96, :], in0=PSB[pa][64:96, :], in1=tabC_own[64:96, c0:c0 + 512], op=ALU.mult)),
                         reads=[psk(pa), "tabC_own"], writes=["t1", "wz1"])
                    P.op("dve", (lambda e: e.tensor_tensor(out=t2[64:96, :], in0=PSB[pa][96:128, :], in1=tabS_own[64:96, c0:c0 + 512], op=ALU.mult)),
                         reads=[psk(pa), "tabS_own"], writes=["t2", "wz1"])
                    P.op("dve", (lambda e: e.tensor_tensor(out=Qt[hb][64:96, c0:c0 + 512], in0=t1[64:96, :], in1=t2[64:96, :], op=ALU.add)),
                         reads=["t1", "t2"], writes=[("qt", hb), "wz0"])

                for blk in range(4):
                    chunks.append(lambda blk=blk: q_chunk(blk))
                return chunks

            steps = [(h, qb, p) for h in range(8) for qb in range(4) for p in range(17)]

            def s_mm(i):
                h, qb, p = steps[i]
                hb = h % 2
                sp_ = i % 2
                for half in range(2):
                    tl = 2 * p + half
                    bank = 2 * sp_ + half
                    P.op("pe", (lambda e, tl=tl, bank=bank, hb=hb, qb=qb: e.matmul(PSB[bank], lhsT=Kt[hb][0:96, tl * 128:(tl + 1) * 128], rhs=Qt[hb][0:96, qb * 512:(qb + 1) * 512], start=True, stop=True)),
                         reads=[("ktn", hb), ("ktr", hb, (tl // 4) * 4 if tl < 32 else 32), ("qt", hb)], writes=[("spair", sp_)])

            norm_pending = []

            def norm_chunks(h, qb, ob):
                hb = h % 2
                orow = slice(0, 64) if hb == 0 else slice(64, 128)
                lrow = slice(64, 128) if hb == 0 else slice(0, 64)
                g = (h * 4 + qb) % 2
                c = h // 2
                out = []
                for q4 in range(4):
                    cs = slice(q4 * 128, (q4 + 1) * 128)
                    out.append(lambda cs=cs, q4=q4: P.op("dve", (lambda e: e.reciprocal(out=recip[g][lrow, cs], in_=PSB[ob][lrow, cs])),
                                                         reads=[psk(ob)], writes=[("recip", g, q4)]))
                out.append(lambda: P.op("dve", (lambda e: e.tensor_tensor(out=tn[g][orow, :], in0=PSB[ob][orow, :], in1=recip[g][lrow, :], op=ALU.mult)),
                                        reads=[psk(ob)] + [("recip", g, q4) for q4 in range(4)], writes=[("tn", g)]))
                out.append(lambda: P.op("pool", (lambda e: e.tensor_tensor(out=maT[orow, c, qb * 512:(qb + 1) * 512], in0=tn[g][orow, :], in1=maT[orow, c, qb * 512:(qb + 1) * 512], op=ALU.mult)),
                                        reads=[("tn", g), ("mT", 0, c, qb)], writes=[("mT", 0, c, qb)]))
                return out

            def emit_pv(i):
                h, qb, p = steps[i]
                hb = h % 2
                pt = PT[i % 3]
                ptk = ("pt", i % 3)
                ob = 4 + (h * 4 + qb) % 2
                for half in range(2):
                    tl = 2 * p + half
                    P.op("pe", (lambda e, tl=tl, half=half: e.matmul(PSB[ob], lhsT=Vext[hb][:, tl, :], rhs=pt[:, half * 512:(half + 1) * 512], start=(tl == 0), stop=(tl == NTILE - 1))),
                         reads=[ptk, ("vextv", hb), ("vext1", hb)], writes=[psk(ob)])
                if p == 16:
                    norm_pending.extend(norm_chunks(h, qb, ob))

            for which in range(2):
                dstT = maT if which == 0 else mbT
                for c in range(4):
                    P.op("act", (lambda e, dstT=dstT, c=c: e.activation(out=dstT[:, c, 0:ZBLK * 512], in_=dstT[:, c, 0:ZBLK * 512], func=AF.Silu)),
                         reads=[("mT", which, c, blk) for blk in range(ZBLK)], writes=[("mT", which, c, blk) for blk in range(ZBLK)])
            b0 = build_chunks(0)
            for _ in range(9):
                b0.pop(0)()
            for _ in range(4):
                b0.pop(-4 + _)()
            P.barrier()
            P.op("pool", lambda e: e.memset(Vext[0][:, :, 64:128], 1.0), writes=[("vext1", 0)])
            P.op("pool", lambda e: e.memset(Vext[1][:, :, 0:64], 1.0), writes=[("vext1", 1)])
            BB[:] = [6, 7, 4, 5]
            bbi[0] = 0
            for grp in p2groups:
                grp()
                for _ in range(4):
                    if b0:
                        b0.pop(0)()
            while b0:
                b0.pop(0)()
            BB[:] = [6, 7]
            bbi[0] = 0
            pending = []
            s_mm(0)
            for i, (h, qb, p) in enumerate(steps):
                hb = h % 2
                sp_ = i % 2
                if qb == 0 and p == 0 and h + 1 < 8:
                    pending = build_chunks(h + 1)
                if qb == 1 and p == 0 and h == 7:
                    pending = pair_chunks(0, nextbb,
                                          extra_q=[("ckvn", s) for s in range(0, NTILE, 4)],
                                          extra_k=[("ktn", 0)] + [("ktr", 0, s) for s in range(0, NTILE, 4)])
                if qb == 0 and p == 0 and h == 7:
                    P.dma("pool", NAW[0], win_v[:, :, 672:1184], writes=["wvb"] + [("cqn", s) for s in (0, 4, 8, 12)])
                    P.dma("pool", NAW[1], win_v[:, :, 1952:2464], writes=["wqb", "tabC_own"])
                    P.dma("pool", NAW[2], win_v[:, :, 160:672], writes=["wkb", "tabS_own"])
                if i + 1 < len(steps):
                    s_mm(i + 1)
                if i >= 1:
                    emit_pv(i - 1)
                if pending and (len(pending) > (67 - (qb * 17 + p)) // 2 or p % 2 == 0):
                    pending.pop(0)()
                pt = PT[i % 3]
                ptk = ("pt", i % 3)
                P.op("act", (lambda e, sp_=sp_, pt=pt: e.activation(out=pt, in_=PS2[sp_], func=AF.Exp, scale=MLA_SCALE)),
                     reads=[("spair", sp_)], writes=[ptk])
                if norm_pending:
                    norm_pending.pop(0)()
            emit_pv(len(steps) - 1)
            assert not pending
            while norm_pending:
                norm_pending.pop(0)()

        if debug_phase is None or debug_phase >= 3:
            phase3()
        if debug_phase == 3:
            for c in range(4):
                dbg_items.append((maT[:, c, 0:1024], 1024, [("mT", 0, c, 0), ("mT", 0, c, 1)]))

        W5 = Bump(AR, ARENA_BYTES - 12288, ARENA_BYTES)
        wgs = []
        for _ in range(2):
            wgs.append(dict(ga=W5.bf(8 * 128).rearrange("p (k n) -> p k n", k=8), gb=W5.bf(8 * 128).rearrange("p (k n) -> p k n", k=8),
                            oa=W5.bf(4 * 128).rearrange("p (k n) -> p k n", k=4), ob=W5.bf(4 * 128).rearrange("p (k n) -> p k n", k=4)))
        woa_v = woa_d.rearrange("(k p) n -> p k n", p=128)
        wob_v = wob_d.rearrange("(k p) n -> p k n", p=128)

        def load_w(c):
            wb = wgs[c % 2]
            wk = ("wg", c % 2)
            P.dma("pool", wb["ga"], win_v[:, :, 2976 + c * 128:2976 + (c + 1) * 128], writes=[(wk, "ga")])
            P.dma("pool", wb["gb"], win_v[:, :, 4000 + c * 128:4000 + (c + 1) * 128], writes=[(wk, "gb")])
            P.dma("pool", wb["oa"], woa_v[:, :, c * 128:(c + 1) * 128], writes=[(wk, "oa")])
            P.dma("pool", wb["ob"], wob_v[:, :, c * 128:(c + 1) * 128], writes=[(wk, "ob")])

        def phase4():
            P.barrier()
            TX.reset()
            TX.off = XREG_END
            TX.limit = ARENA_BYTES - 12288
            Bt = [Bt0, TX.bf(NVAR * 128)]
            vbext = TX.bf(NKEEP * 8 * 128).rearrange("p (t hp two c) -> p t hp two c", t=NKEEP, hp=4, two=2, c=128)
            wvb, wqb, wkb = NAW
            PTn = [TX.bf(896) for _ in range(3)]
            recip = [TX.f32(512) for _ in range(2)]
            tn = [TX.f32(512) for _ in range(2)]
            P.dma("pool", Bt[0], nab_d[0], writes=[("bt", 0)])
            bbi = [0]

            def nextbb():
                b_ = 6 + bbi[0] % 2
                bbi[0] += 1
                return b_

            def build_v(t):
                P.op("pool", (lambda e: e.memset(vbext[:, t, :, 0, 64:128], 1.0)), writes=[("vb1", t)])
                P.op("pool", (lambda e: e.memset(vbext[:, t, :, 1, 0:64], 1.0)), writes=[("vb1", t)])
                pb = nextbb()
                for k in range(8):
                    P.op("pe", (lambda e, k=k: e.matmul(PSB[pb], lhsT=hT_keep[:, k, t * 128:(t + 1) * 128], rhs=wvb[:, k, :], start=(k == 0), stop=(k == 7))),
                         reads=["wvb", ("hk", t)], writes=[psk(pb)])
                pv = PSB[pb].rearrange("p (hp two c) -> p hp two c", hp=4, two=2, c=64)
                P.op("dve", (lambda e: e.tensor_copy(out=vbext[:, t, :, 0, 0:64], in_=pv[:, :, 0, :])), reads=[psk(pb)], writes=[("vbv", t)])
                P.op("dve", (lambda e: e.tensor_copy(out=vbext[:, t, :, 1, 64:128], in_=pv[:, :, 1, :])), reads=[psk(pb)], writes=[("vbv", t)])

            def build_pair(hp):
                for ch in pair_chunks(hp, nextbb):
                    ch()

            steps = [(h, jl) for h in range(8) for jl in range(16)]

            def slots_of(jl):
                kts = [0, 1, 2, 3] if jl < 2 else list(range(jl - 2, jl + 3))
                v0 = 0 if jl == 0 else (4 if jl == 1 else 8)
                return kts, v0

            def s_mm(i):
                h, jl = steps[i]
                hb = h % 2
                pb_ = (h // 2) % 2
                rs = slice(64 * (h % 2), 64 * (h % 2) + 64)
                sreg = PS2[i % 2]
                sk = ("spair", i % 2)
                kts, v0 = slots_of(jl)
                nw = len(kts)
                rdb = [("bt", hb), "identb"]
                rdq = [("qbT", pb_), ("kbT", pb_)]
                qv = qbT2[pb_][rs, jl * 128:(jl + 1) * 128]
                P.op("pe", (lambda e: e.matmul(sreg[:, 0:512], lhsT=identb, rhs=Bt[hb][:, v0 * 128:v0 * 128 + 512], start=True, stop=False, skip_group_check=True)),
                     reads=rdb, writes=[sk])
                if nw == 5:
                    P.op("pe", (lambda e: e.matmul(sreg[:, 512:640], lhsT=identb, rhs=Bt[hb][:, (v0 + 4) * 128:(v0 + 5) * 128], start=True, stop=False, skip_group_check=True)),
                         reads=rdb, writes=[sk])
                slots = kts + [18, 19]
                for si, kt in enumerate(slots):
                    first_ctx_boundary = (nw == 4 and si == 4)
                    P.op("pe", (lambda e, si=si, kt=kt, fc=first_ctx_boundary: e.matmul(sreg[:, si * 128:(si + 1) * 128], lhsT=kbT2[pb_][rs, kt * 128:(kt + 1) * 128], rhs=qv, start=fc, stop=True, skip_group_check=True)),
                         reads=rdq, writes=[sk])

            na_norm = []
            na_pair = []

            def emit_pv_na(i):
                h, jl = steps[i]
                hb = h % 2
                jg, jj = jl // 4, jl % 4
                kts, v0 = slots_of(jl)
                slots = kts + [18, 19]
                ns = len(slots)
                pt = PTn[i % 3]
                ptk = ("ptn", i % 3)
                ob = 4 + (h * 4 + jg) % 2
                for si, kt in enumerate(slots):
                    P.op("pe", (lambda e, si=si, kt=kt: e.matmul(PSB[ob][:, jj * 128:(jj + 1) * 128], lhsT=vbext[:, kt, h // 2, h % 2, :], rhs=pt[:, si * 128:(si + 1) * 128], start=(si == 0), stop=(si == ns - 1), skip_group_check=True)),
                         reads=[ptk, ("vbv", kt), ("vb1", kt)], writes=[psk(ob)])
                if jj == 3:
                    orow = slice(0, 64) if hb == 0 else slice(64, 128)
                    lrow = slice(64, 128) if hb == 0 else slice(0, 64)
                    g = (h * 4 + jg) % 2
                    c = h // 2
                    for q4 in range(4):
                        cs = slice(q4 * 128, (q4 + 1) * 128)
                        na_norm.append(lambda cs=cs, q4=q4: P.op("dve", (lambda e: e.reciprocal(out=recip[g][lrow, cs], in_=PSB[ob][lrow, cs])),
                                                               reads=[psk(ob)], writes=[("recip", g, q4)]))
                    na_norm.append(lambda: P.op("dve", (lambda e: e.tensor_tensor(out=tn[g][orow, :], in0=PSB[ob][orow, :], in1=recip[g][lrow, :], op=ALU.mult)),
                                                reads=[psk(ob)] + [("recip", g, q4) for q4 in range(4)], writes=[("tn", g)]))
                    na_norm.append(lambda: P.op("pool", (lambda e: e.tensor_tensor(out=mbT[orow, c, jg * 512:(jg + 1) * 512], in0=tn[g][orow, :], in1=mbT[orow, c, jg * 512:(jg + 1) * 512], op=ALU.mult)),
                                                reads=[("tn", g), ("mT", 1, c, jg)], writes=[("mT", 1, c, jg)]))

            for t in (0, 1, 2, 3, 4, 18, 19):
                build_v(t)
            s_mm(0)
            for i, (h, jl) in enumerate(steps):
                hb = h % 2
                jg, jj = jl // 4, jl % 4
                if i + 1 < len(steps):
                    s_mm(i + 1)
                if i >= 1:
                    emit_pv_na(i - 1)
                if h == 0 and jl + 5 <= 17:
                    build_v(jl + 5)
                kts, v0 = slots_of(jl)
                slots = kts + [18, 19]
                ns = len(slots)
                pt = PTn[i % 3]
                ptk = ("ptn", i % 3)
                sreg = PS2[i % 2]
                P.op("act", (lambda e, pt=pt, sreg=sreg, ns=ns: e.activation(out=pt[:, 0:ns * 128], in_=sreg[:, 0:ns * 128], func=AF.Exp)),
                     reads=[("spair", i % 2)], writes=[ptk])
                for _ in range(2):
                    if na_norm:
                        na_norm.pop(0)()
                if jl == 8 and h == 6:
                    load_w(0)
                if jl == 8 and h == 7:
                    load_w(1)
                if jl == 4 and h + 1 < 8:
                    P.dma("pool", Bt[(h + 1) % 2], nab_d[h + 1], writes=[("bt", (h + 1) % 2)])
                    if h % 2 == 0 and h + 2 < 8:
                        na_pair.extend(pair_chunks(h // 2 + 1, nextbb))
                for _ in range(2):
                    if na_pair:
                        na_pair.pop(0)()
            emit_pv_na(len(steps) - 1)
            assert not na_pair
            while na_norm:
                na_norm.pop(0)()

        if debug_phase is None or debug_phase >= 4:
            phase4()
        if debug_phase == 4:
            for c in range(4):
                dbg_items.append((mbT[:, c, 0:1024], 1024, [("mT", 1, c, 0), ("mT", 1, c, 1)]))
                dbg_items.append((mbT[:, c, 1024:2048], 1024, [("mT", 1, c, 2), ("mT", 1, c, 3)]))

        def phase5():
            P.barrier()
            TX.reset()
            TX.limit = ARENA_BYTES - 12288
            mT = TX.bf(8 * NOWN).rearrange("p (c t) -> p c t", c=8)
            wout = TX.bf(8 * D).rearrange("p (k n) -> p k n", k=8)
            woutf = TX.f32(2 * D).rearrange("p (k n) -> p k n", k=2)
            sga = [TX.bf(512) for _ in range(2)]
            sgb = [TX.bf(512) for _ in range(2)]
            t1 = [TX.f32(512) for _ in range(2)]
            t2 = [TX.f32(512) for _ in range(2)]
            xt = [TX.f32(D) for _ in range(3)]
            rr = [TX.f32(D) for _ in range(3)]
            junk = TX.bf(D)
            ot = [TX.f32(D) for _ in range(2)]
            dg = [TX.f32(128) for _ in range(2)]

            wout_v = wout_d.rearrange("(k p) n -> p k n", p=128)
            gstg = TX.f32(8 * 256).rearrange("p (k n) -> p k n", k=8)
            P.dma("sp", rr[1][0:1, :], bgate_d[:, :], writes=[(("rr", 1), 0), (("rr", 1), 1)])

            gstg_flat = gstg.rearrange("p k n -> p (k n)")
            gbuf = [gstg_flat[:, 0:1024], gstg_flat[:, 1024:2048]]
            gacc = rr[0]

            def gate_load(k):
                P.dma("sp", gbuf[k % 2], wmod_d[k * 128:(k + 1) * 128, 2048:3072], writes=[("gbuf", k % 2)])

            def gate_acc(k):
                if k == 0:
                    P.op("dve", (lambda e: e.tensor_scalar(out=gacc, in0=gbuf[0], scalar1=sT[:, 0, 0:1], scalar2=None, op0=ALU.mult)),
                         reads=[("gbuf", 0), "sT"], writes=[(("rr", 0), 0), (("rr", 0), 1)])
                else:
                    P.op("dve", (lambda e: e.scalar_tensor_tensor(out=gacc, in0=gbuf[k % 2], scalar=sT[:, k, 0:1], in1=gacc, op0=ALU.mult, op1=ALU.add)),
                         reads=[("gbuf", k % 2), "sT", (("rr", 0), 0), (("rr", 0), 1)], writes=[(("rr", 0), 0), (("rr", 0), 1)])

            def gate_bias():
                P.op("dve", (lambda e: e.tensor_tensor(out=gacc[0:1, :], in0=gacc[0:1, :], in1=rr[1][0:1, :], op=ALU.add)),
                     reads=[(("rr", 0), 0), (("rr", 0), 1), (("rr", 1), 0), (("rr", 1), 1)], writes=[(("rr", 0), 0), (("rr", 0), 1)])

            def gate_finish():
                for n in range(2):
                    P.op("pe", (lambda e, n=n: e.matmul(PSB[6 + n], lhsT=ones128, rhs=gacc[:, n * 512:(n + 1) * 512], start=True, stop=True)),
                         reads=["ones128", (("rr", 0), 0), (("rr", 0), 1)], writes=[psk(6 + n)])
                    P.op("dve", (lambda e, n=n: e.tensor_copy(out=gate_b[:, n * 512:(n + 1) * 512], in_=PSB[6 + n])),
                         reads=[psk(6 + n)], writes=[("gate_b", n)])

            def fold_step(k):
                if k % 2 == 0:
                    P.dma("sp", woutf, wout_v[:, k:k + 2, :], writes=["woutf"])
                P.op("dve", (lambda e: e.tensor_tensor(out=wout[:, k, :], in0=woutf[:, k % 2, :], in1=gate_b, op=ALU.mult)),
                     reads=["woutf", ("gate_b", 0), ("gate_b", 1)], writes=[("wout", k)])

            fold_q = []

            for c in range(8):
                wb = wgs[c % 2]
                wk = ("wg", c % 2)
                if c < 4:
                    gate_load(2 * c)
                    gate_load(2 * c + 1)
                for blk in range(4):
                    i = c * 4 + blk
                    pya = (i % 2) * 2
                    pyb = (i % 2) * 2 + 1
                    cs = slice(blk * 512, (blk + 1) * 512)
                    hk = [("hk", s) for s in range(blk * 4, blk * 4 + 4)]
                    for k in range(4):
                        P.op("pe", (lambda e, k=k, wb=wb, cs=cs, pya=pya: e.matmul(PSB[pya], lhsT=wb["oa"][:, k, :], rhs=maT[:, k, cs], start=(k == 0), stop=(k == 3))),
                             reads=[(wk, "oa"), ("mT", 0, k, blk)], writes=[psk(pya)])
                    for k in range(4):
                        P.op("pe", (lambda e, k=k, wb=wb, cs=cs, pyb=pyb: e.matmul(PSB[pyb], lhsT=wb["ob"][:, k, :], rhs=mbT[:, k, cs], start=(k == 0), stop=(k == 3))),
                             reads=[(wk, "ob"), ("mT", 1, k, blk)], writes=[psk(pyb)])
                    for k in range(8):
                        P.op("pe", (lambda e, k=k, wb=wb, cs=cs: e.matmul(PSB[4], lhsT=wb["ga"][:, k, :], rhs=hT_keep[:, k, cs], start=(k == 0), stop=(k == 7))),
                             reads=[(wk, "ga")] + hk, writes=[psk(4)])
                    P.op("act", (lambda e, i=i: e.activation(out=sga[i % 2], in_=PSB[4], func=AF.Sigmoid)), reads=[psk(4)], writes=[("sga", i % 2)])
                    for k in range(8):
                        P.op("pe", (lambda e, k=k, wb=wb, cs=cs: e.matmul(PSB[5], lhsT=wb["gb"][:, k, :], rhs=hT_keep[:, k, cs], start=(k == 0), stop=(k == 7))),
                             reads=[(wk, "gb")] + hk, writes=[psk(5)])
                    P.op("act", (lambda e, i=i: e.activation(out=sgb[i % 2], in_=PSB[5], func=AF.Sigmoid)), reads=[psk(5)], writes=[("sgb", i % 2)])
                    P.op("dve", (lambda e, i=i, pya=pya: e.tensor_tensor(out=t1[i % 2], in0=PSB[pya], in1=sga[i % 2], op=ALU.mult)),
                         reads=[psk(pya), ("sga", i % 2)], writes=[("t1", i % 2)])
                    P.op("dve", (lambda e, i=i, pyb=pyb: e.tensor_tensor(out=t2[i % 2], in0=PSB[pyb], in1=sgb[i % 2], op=ALU.mult)),
                         reads=[psk(pyb), ("sgb", i % 2)], writes=[("t2", i % 2)])
                    P.op("pool", (lambda e, i=i, c=c, cs=cs: e.tensor_tensor(out=mT[:, c, cs], in0=t1[i % 2], in1=t2[i % 2], op=ALU.add)),
                         reads=[("t1", i % 2), ("t2", i % 2)], writes=[("mm", c, blk)])
                    if fold_q:
                        fold_step(fold_q.pop(0))
                    if c < 4 and blk in (1, 3):
                        gate_acc(2 * c + blk // 2)
                    if c == 4 and blk == 0:
                        gate_bias()
                        gate_finish()
                        fold_q.extend(range(8))
                if c + 2 < 8:
                    load_w(c + 2)

            def y_front(jl):
                xb = xt[jl % 3]
                xk = ("xt5", jl % 3)
                rb = rr[jl % 3]
                rk = ("rr", jl % 3)
                for n in range(2):
                    for c in range(8):
                        P.op("pe", (lambda e, c=c, n=n: e.matmul(PSB[6 + n], lhsT=mT[:, c, jl * 128:(jl + 1) * 128], rhs=wout[:, c, n * 512:(n + 1) * 512], start=(c == 0), stop=(c == 7))),
                             reads=[("wout", c), ("mm", c, jl // 4)], writes=[psk(6 + n)])
                    P.op("dve", (lambda e, n=n: e.tensor_tensor(out=rb[:, n * 512:(n + 1) * 512], in0=PSB[6 + n], in1=xb[:, n * 512:(n + 1) * 512], op=ALU.add)),
                         reads=[psk(6 + n), xk], writes=[(rk, n)])
                sc_ = NTILE + jl
                P.op("act", (lambda e: e.activation(out=junk, in_=rb, func=AF.Square, accum_out=ssq[:, sc_:sc_ + 1])),
                     reads=[(rk, 0), (rk, 1)], writes=["junk5", ("ssq", sc_)])
                P.op("pool", (lambda e: e.tensor_scalar(out=rsv[:, sc_:sc_ + 1], in0=ssq[:, sc_:sc_ + 1], scalar1=1.0 / D, scalar2=EPS, op0=ALU.mult, op1=ALU.add)),
                     reads=[("ssq", sc_)], writes=[("rsv", sc_)])
                P.op("pool", (lambda e: e.tensor_tensor(out=rstd[:, sc_:sc_ + 1], in0=rsv[:, sc_:sc_ + 1], in1=mhalf[:, 0:1], op=ALU.pow)),
                     reads=[("rsv", sc_), "mhalf"], writes=[("rstd", sc_)])

            def y_back(jl):
                rb = rr[jl % 3]
                rk = ("rr", jl % 3)
                sc_ = NTILE + jl
                ob_ = ot[jl % 2]
                ok = ("ot", jl % 2)
                P.op("dve", (lambda e: e.scalar_tensor_tensor(out=ob_, in0=rb, scalar=rstd[:, sc_:sc_ + 1], in1=fg_b, op0=ALU.mult, op1=ALU.mult)),
                     reads=[(rk, 0), (rk, 1), ("rstd", sc_), "fg_b"], writes=[ok])
                P.dma("sp", out_d[jl * 128:(jl + 1) * 128, :], ob_, reads=[ok], is_output=True)

            for jl in range(3):
                P.dma("sp", xt[jl % 3], xs[jl * 128:(jl + 1) * 128, :], writes=[("xt5", jl % 3)])
            for it in range(17):
                if it < 16:
                    y_front(it)
                    if it + 3 < 16:
                        P.dma("sp", xt[(it + 3) % 3], xs[(it + 3) * 128:(it + 4) * 128, :], writes=[("xt5", (it + 3) % 3)])
                if it >= 1:
                    y_back(it - 1)

        if debug_phase is None:
            phase5()

        if debug_phase is not None:
            P.barrier()
            T.reset()
            col = 0
            stgs = [T.f32(1024) for _ in range(2)]
            ns = 0
            for ap2, n, rd in dbg_items:
                for o in range(0, n, 1024):
                    m = min(1024, n - o)
                    stg = stgs[ns % 2][:, 0:m]
                    sk_ = ("dbgs", ns % 2)
                    ns += 1
                    P.op("dve", (lambda e, ap2=ap2, stg=stg, o=o, m=m: e.tensor_copy(out=stg, in_=ap2[:, o:o + m])), reads=rd, writes=[sk_])
                    P.dma("sp", dbg_d[:, col:col + m], stg, reads=[sk_], is_output=True)
                    col += m
            assert col <= DBGN

        run = P.emit(sems, dsems)

        @block.sync
        def _(e):
            run("sp", e, final_wait=True)

        @block.scalar
        def _(e):
            run("act", e)

        @block.vector
        def _(e):
            run("dve", e)

        @block.gpsimd
        def _(e):
            run("pool", e)

        @block.tensor
        def _(e):
            run("pe", e)
    return nc


def _gtile(hf, L):
    return L if hf == 0 else 31 - L


def _stage_tiles(hf):
    tiles = []
    for s in range(NTILE):
        if s < 18:
            tiles.append(("g", _gtile(hf, s)))
        elif s < 20:
            tiles.append(("c", s - 18))
        else:
            tiles.append(("g", _gtile(hf, s - 2)))
    return tiles


def _rope_tables(hf):
    half = 8
    freqs = np.float32(10000.0) ** (-(np.arange(half, dtype=np.float32) / np.float32(half)))
    tC = np.ones((32, NKEY), np.float32)
    tS = np.zeros((32, NKEY), np.float32)
    for s, (kind, g) in enumerate(_stage_tiles(hf)):
        if kind != "g":
            continue
        t = g * 128 + np.arange(128)
        row = (t // GRID_W).astype(np.float32)
        col = (t % GRID_W).astype(np.float32)
        for d in range(32):
            pos = row if d < 16 else col
            ang = (pos * freqs[d % 8]).astype(np.float32)
            sgn = -1.0 if (d % 16) < 8 else 1.0
            tC[d, s * 128:(s + 1) * 128] = np.cos(ang)
            tS[d, s * 128:(s + 1) * 128] = sgn * np.sin(ang)
    return tC, tS


_PARTNER = np.array([d + 8 if (d % 16) < 8 else d - 8 for d in range(32)])


def _na_bias(rpb, hf):
    out = np.full((8, 128, NVAR * 128), NEG, np.float32)
    variants = [(0, kt) for kt in range(4)] + [(1, kt) for kt in range(4)] + [(5, 3 + i) for i in range(5)]
    p = np.arange(128)
    for v, (jl, kt) in enumerate(variants):
        gq = _gtile(hf, jl)
        gk = _gtile(hf, kt)
        r = 2 * gq + p // 64
        c = p % 64
        kr = 2 * gk + p // 64
        kc = p % 64
        r_start = np.clip(r - 4, 0, 56)
        c_start = np.clip(c - 8, 0, 48)
        dr = kr[:, None] - r[None, :]
        dc = kc[:, None] - c[None, :]
        valid = (kr[:, None] >= r_start[None, :]) & (kr[:, None] < r_start[None, :] + 8) & \
                (kc[:, None] >= c_start[None, :]) & (kc[:, None] < c_start[None, :] + 16)
        ri = np.clip(dr + 7, 0, 14)
        ci = np.clip(dc + 15, 0, 30)
        for h in range(8):
            g = rpb[h][ri, ci]
            out[h, :, v * 128:(v + 1) * 128] = np.where(valid, g, np.float32(NEG))
    return out


def prep_core(inputs, core):
    b, hf = core // 2, core % 2
    f = lambda a: np.ascontiguousarray(np.asarray(a, dtype=np.float32))
    x = np.asarray(inputs["x"]); ctx = np.asarray(inputs["ctx"])
    parts = []
    for kind, g in _stage_tiles(hf):
        parts.append(x[b, g * 128:(g + 1) * 128] if kind == "g" else ctx[b, g * 128:(g + 1) * 128])
    xs = np.concatenate(parts, axis=0)
    c = np.asarray(inputs["c"])[b]; c_ctx = np.asarray(inputs["c_ctx"])
    cT = np.stack([c.reshape(8, 128).T, c_ctx.reshape(8, 128).T], axis=2).reshape(128, 16)
    w_in = np.asarray(inputs["w_in"])[0]
    w_uq = np.asarray(inputs["w_uq"])[0]
    wuqx = np.concatenate([np.concatenate([w_uq[:, h * 96:(h + 1) * 96], w_uq[:, h * 96 + 64 + _PARTNER]], axis=1) for h in range(8)], axis=1)
    tC, tS = _rope_tables(hf)
    b_mod = np.asarray(inputs["b_mod"])[0]
    return {
        "xs": f(xs),
        "cT": f(cT),
        "wmod": f(np.asarray(inputs["w_mod"])[0]),
        "bcol": f(b_mod.reshape(24, 128).T),
        "bgate": f(b_mod[2048:3072].reshape(1, D)),
        "gT": f(np.asarray(inputs["norm_g"])[0].reshape(8, 128).T),
        "w_in": f(w_in),
        "wkrsw": f(w_in[:, 128 + _PARTNER]),
        "gckv": f(np.asarray(inputs["g_ckv"])[0].reshape(128, 1)),
        "gcq": f(np.asarray(inputs["g_cq"])[0].reshape(2, 128).T),
        "wuq": f(wuqx),
        "wukv": f(np.asarray(inputs["w_ukv"])[0]),
        "tabC": f(tC),
        "tabS": f(tS),
        "nab": f(_na_bias(np.asarray(inputs["rpb"])[0], hf)),
        "woa": f(np.asarray(inputs["w_oa"])[0]),
        "wob": f(np.asarray(inputs["w_ob"])[0]),
        "wout": f(np.asarray(inputs["w_out"])[0]),
        "fgb": f(np.broadcast_to(np.asarray(inputs["final_g"])[None, :], (128, D))),
        "ident": np.eye(128, dtype=np.float32),
    }


def kernel(**inputs):
    nc = build_nc(None)
    in_maps = [prep_core(inputs, core) for core in range(8)]
    res = run_bass_kernel_spmd(nc, in_maps, core_ids=list(range(8)))
    out = np.zeros((4, SEQ, D), np.float32)
    for core in range(8):
        b, hf = core // 2, core % 2
        o = np.asarray(res.results[core]["out"])
        for jl in range(16):
            g = _gtile(hf, jl)
            out[b, g * 128:(g + 1) * 128] = o[jl * 128:(jl + 1) * 128]
    return out
```
